# Optimizing a Trainium2 kernel written in Bass

```python
import math
import jax, jax.numpy as jnp
from jax import lax
import numpy as np

D_MODEL = 2048
BATCH = 1
SEQ = 8192
DEPTH = 1

D_MIX = D_MODEL
D_HYENA = D_MIX // 2
D_ATTN = D_MIX - D_HYENA
N_DIFF_HEADS = 8
DIFF_HEAD_DIM = D_ATTN // (2 * N_DIFF_HEADS)
VAL_DIM = 2 * DIFF_HEAD_DIM
D_QK = 2 * N_DIFF_HEADS * DIFF_HEAD_DIM
D_IN = 3 * D_HYENA + 2 * D_QK + N_DIFF_HEADS * VAL_DIM
ROT_DIM = DIFF_HEAD_DIM // 4
ROPE_THETA = 500000.0
Q_BLOCK = 128
HYENA_ORDER = 2
SHORT_CONV = 3
FILTER_EMB = 33
FILTER_HIDDEN = 64
N_DIRS = 2
DECAY_TARGET = 1e-2
FAST_DECAY_PCT = 0.3
SLOW_DECAY_PCT = 1.5
MIN_DECAY = math.log(DECAY_TARGET) / SLOW_DECAY_PCT
MAX_DECAY = math.log(DECAY_TARGET) / FAST_DECAY_PCT
D_FF = 5632
LN_EPS = 1e-5
ALPHA = (2.0 * DEPTH) ** 0.25
BETA = (8.0 * DEPTH) ** -0.25

kernel_name = "hybrid_hyena_diffattn_macaron_encoder"


def lambda_init_for(layer_idx):
    return 0.8 - 0.6 * math.exp(-0.3 * (layer_idx - 1))


def layer_norm(x, g, b):
    xf = x.astype(jnp.float32)
    mu = jnp.mean(xf, axis=-1, keepdims=True)
    var = jnp.mean(jnp.square(xf - mu), axis=-1, keepdims=True)
    return ((xf - mu) * lax.rsqrt(var + LN_EPS)).astype(x.dtype) * g + b


def rms_norm(x, g):
    xf = x.astype(jnp.float32)
    return (xf * lax.rsqrt(jnp.mean(jnp.square(xf), axis=-1, keepdims=True) + LN_EPS)).astype(x.dtype) * g


def swiglu(x, w_gate, w_up, w_down):
    return (jax.nn.silu(x @ w_gate) * (x @ w_up)) @ w_down


def short_conv(u, w, b):
    L = u.shape[1]
    up = jnp.pad(u, ((0, 0), (1, 1), (0, 0)))
    return up[:, :L] * w[0] + up[:, 1:L + 1] * w[1] + up[:, 2:] * w[2] + b


def hyena_filters(L, w1, b1, w2, b2, w3, b3, freq, w_out):
    t = jnp.linspace(0.0, 1.0, L, dtype=jnp.float32)[:, None]
    bands = (FILTER_EMB - 1) // 2
    w = 2.0 * math.pi * jnp.arange(L, dtype=jnp.float32)[:, None] / L
    f = jnp.linspace(1e-4, bands - 1, bands, dtype=jnp.float32)[None, :]
    z = jnp.concatenate([t, jnp.cos(f * w), -jnp.sin(f * w)], axis=-1)
    h = jnp.sin(freq * (z @ w1 + b1))
    h = jnp.sin(freq * (h @ w2 + b2))
    h = jnp.sin(freq * (h @ w3 + b3))
    h = (h @ w_out).astype(jnp.float32)
    deltas = jnp.abs(jnp.linspace(MIN_DECAY, MAX_DECAY, D_HYENA, dtype=jnp.float32))
    decay = jnp.exp(-t * deltas)
    return h.reshape(L, HYENA_ORDER, N_DIRS, D_HYENA) * decay[:, None, None, :]


def bidir_fftconv(u, h_fwd, h_bwd, bias):
    L = u.shape[1]
    k = jnp.concatenate([h_fwd, jnp.zeros_like(h_fwd[:1]), h_bwd[:0:-1]], axis=0)
    k_f = jnp.fft.rfft(k, n=2 * L, axis=0)
    u_f = jnp.fft.rfft(u.astype(jnp.float32), n=2 * L, axis=1)
    y = jnp.fft.irfft(u_f * k_f[None], n=2 * L, axis=1)[:, :L]
    return (y + u.astype(jnp.float32) * bias.astype(jnp.float32)).astype(u.dtype)


def hyena_group(p_hy, conv_w, conv_b, fw1, fb1, fw2, fb2, fw3, fb3, ffreq, fw_out, hyena_bias, hyena_norm_g):
    L = p_hy.shape[1]
    z = short_conv(p_hy, conv_w, conv_b)
    v, g1, g2 = jnp.split(z, 3, axis=-1)
    filt = hyena_filters(L, fw1, fb1, fw2, fb2, fw3, fb3, ffreq, fw_out)
    gates = (g1, g2)
    y = v
    for n in range(HYENA_ORDER):
        y = gates[n] * bidir_fftconv(y, filt[:, n, 0], filt[:, n, 1], hyena_bias[n])
    return rms_norm(y, hyena_norm_g)


def partial_rope(x, pos):
    half = ROT_DIM // 2
    inv = ROPE_THETA ** (-jnp.arange(0, ROT_DIM, 2, dtype=jnp.float32) / ROT_DIM)
    ang = pos[:, None] * inv[None, :]
    cos = jnp.concatenate([jnp.cos(ang), jnp.cos(ang)], axis=-1)[None, :, None, None, :]
    sin = jnp.concatenate([jnp.sin(ang), jnp.sin(ang)], axis=-1)[None, :, None, None, :]
    xr = x[..., :ROT_DIM].astype(jnp.float32)
    rot = jnp.concatenate([-xr[..., half:], xr[..., :half]], axis=-1)
    xr = (xr * cos + rot * sin).astype(x.dtype)
    return jnp.concatenate([xr, x[..., ROT_DIM:]], axis=-1)


def diff_attention_group(p_q, p_k, p_v, lq1, lk1, lq2, lk2, subln_g, lambda_init):
    B, L, _ = p_q.shape
    nb = L // Q_BLOCK
    pos = jnp.arange(L, dtype=jnp.float32)
    q = partial_rope(p_q.reshape(B, L, N_DIFF_HEADS, 2, DIFF_HEAD_DIM), pos) * (DIFF_HEAD_DIM ** -0.5)
    k = partial_rope(p_k.reshape(B, L, N_DIFF_HEADS, 2, DIFF_HEAD_DIM), pos)
    v = p_v.reshape(B, L, N_DIFF_HEADS, VAL_DIM)
    lam = (jnp.exp(jnp.sum(lq1.astype(jnp.float32) * lk1.astype(jnp.float32)))
           - jnp.exp(jnp.sum(lq2.astype(jnp.float32) * lk2.astype(jnp.float32))) + lambda_init)
    qb = q.reshape(B, nb, Q_BLOCK, N_DIFF_HEADS, 2, DIFF_HEAD_DIM).transpose(1, 0, 3, 4, 2, 5)
    kt = k.transpose(0, 2, 3, 1, 4)
    vt = v.transpose(0, 2, 1, 3)

    def block(q_blk):
        s = jnp.einsum('bhmqd,bhmkd->bhmqk', q_blk, kt).astype(jnp.float32)
        a = jax.nn.softmax(s, axis=-1)
        w = a[:, :, 0] - lam * a[:, :, 1]
        return jnp.einsum('bhqk,bhkv->bhqv', w.astype(vt.dtype), vt)

    o = lax.map(block, qb)
    o = o.transpose(1, 0, 3, 2, 4).reshape(B, L, N_DIFF_HEADS, VAL_DIM)
    o = rms_norm(o, subln_g) * (1.0 - lambda_init)
    return o.reshape(B, L, D_ATTN)


def hybrid_mixer(h, w_in, conv_w, conv_b, fw1, fb1, fw2, fb2, fw3, fb3, ffreq, fw_out,
                 hyena_bias, hyena_norm_g, lq1, lk1, lq2, lk2, subln_g, w_out, lambda_init):
    proj = h @ w_in
    s1 = 3 * D_HYENA
    p_hy, p_q, p_k, p_v = jnp.split(proj, [s1, s1 + D_QK, s1 + 2 * D_QK], axis=-1)
    y_hy = hyena_group(p_hy, conv_w, conv_b, fw1, fb1, fw2, fb2, fw3, fb3, ffreq, fw_out,
                       hyena_bias, hyena_norm_g)
    y_at = diff_attention_group(p_q, p_k, p_v, lq1, lk1, lq2, lk2, subln_g, lambda_init)
    return jnp.concatenate([y_hy, y_at], axis=-1) @ w_out


def setup_inputs(seed: int = 0) -> dict:
    key = jax.random.key(seed)
    ks = iter(jax.random.split(key, 40))

    def nrm(shape, scale):
        return jax.random.normal(next(ks), shape, jnp.float32) * scale

    def gain(shape):
        return 1.0 + nrm(shape, 0.05)

    n = DEPTH
    return {
        "x": nrm((BATCH, SEQ, D_MODEL), 1.0),
        "ffn1_w_gate": nrm((n, D_MODEL, D_FF), D_MODEL ** -0.5),
        "ffn1_w_up": nrm((n, D_MODEL, D_FF), D_MODEL ** -0.5),
        "ffn1_w_down": nrm((n, D_FF, D_MODEL), BETA * D_FF ** -0.5),
        "ln1_g": gain((n, D_MODEL)),
        "ln1_b": nrm((n, D_MODEL), 0.02),
        "w_in": nrm((n, D_MODEL, D_IN), D_MODEL ** -0.5),
        "hyena_conv_w": nrm((n, SHORT_CONV, 3 * D_HYENA), 0.5),
        "hyena_conv_b": nrm((n, 3 * D_HYENA), 0.02),
        "filt_w1": nrm((n, FILTER_EMB, FILTER_HIDDEN), FILTER_EMB ** -0.5),
        "filt_b1": nrm((n, FILTER_HIDDEN), 0.1),
        "filt_w2": nrm((n, FILTER_HIDDEN, FILTER_HIDDEN), FILTER_HIDDEN ** -0.5),
        "filt_b2": nrm((n, FILTER_HIDDEN), 0.1),
        "filt_w3": nrm((n, FILTER_HIDDEN, FILTER_HIDDEN), FILTER_HIDDEN ** -0.5),
        "filt_b3": nrm((n, FILTER_HIDDEN), 0.1),
        "filt_freq": gain((n, FILTER_HIDDEN)),
        "filt_w_out": nrm((n, FILTER_HIDDEN, HYENA_ORDER * N_DIRS * D_HYENA), FILTER_HIDDEN ** -0.5),
        "hyena_bias": nrm((n, HYENA_ORDER, D_HYENA), 0.5),
        "hyena_norm_g": gain((n, D_HYENA)),
        "lambda_q1": nrm((n, DIFF_HEAD_DIM), 0.1),
        "lambda_k1": nrm((n, DIFF_HEAD_DIM), 0.1),
        "lambda_q2": nrm((n, DIFF_HEAD_DIM), 0.1),
        "lambda_k2": nrm((n, DIFF_HEAD_DIM), 0.1),
        "subln_g": gain((n, VAL_DIM)),
        "w_out": nrm((n, D_MIX, D_MODEL), BETA * D_MIX ** -0.5),
        "ln2_g": gain((n, D_MODEL)),
        "ln2_b": nrm((n, D_MODEL), 0.02),
        "ffn2_w_gate": nrm((n, D_MODEL, D_FF), D_MODEL ** -0.5),
        "ffn2_w_up": nrm((n, D_MODEL, D_FF), D_MODEL ** -0.5),
        "ffn2_w_down": nrm((n, D_FF, D_MODEL), BETA * D_FF ** -0.5),
        "ln3_g": gain((n, D_MODEL)),
        "ln3_b": nrm((n, D_MODEL), 0.02),
    }


def reference(x, ffn1_w_gate, ffn1_w_up, ffn1_w_down, ln1_g, ln1_b, w_in, hyena_conv_w, hyena_conv_b,
              filt_w1, filt_b1, filt_w2, filt_b2, filt_w3, filt_b3, filt_freq, filt_w_out, hyena_bias,
              hyena_norm_g, lambda_q1, lambda_k1, lambda_q2, lambda_k2, subln_g, w_out, ln2_g, ln2_b,
              ffn2_w_gate, ffn2_w_up, ffn2_w_down, ln3_g, ln3_b):
    for i in range(DEPTH):
        lam_init = lambda_init_for(i + 1)
        x = layer_norm(ALPHA * x + 0.5 * swiglu(x, ffn1_w_gate[i], ffn1_w_up[i], ffn1_w_down[i]),
                       ln1_g[i], ln1_b[i])
        y = hybrid_mixer(x, w_in[i], hyena_conv_w[i], hyena_conv_b[i], filt_w1[i], filt_b1[i],
                         filt_w2[i], filt_b2[i], filt_w3[i], filt_b3[i], filt_freq[i], filt_w_out[i],
                         hyena_bias[i], hyena_norm_g[i], lambda_q1[i], lambda_k1[i], lambda_q2[i],
                         lambda_k2[i], subln_g[i], w_out[i], lam_init)
        x = layer_norm(ALPHA * x + y, ln2_g[i], ln2_b[i])
        x = layer_norm(ALPHA * x + 0.5 * swiglu(x, ffn2_w_gate[i], ffn2_w_up[i], ffn2_w_down[i]),
                       ln3_g[i], ln3_b[i])
    return x
```

```python
import math
from contextlib import ExitStack
import numpy as np
import ml_dtypes
import concourse.bass as bass
import concourse.mybir as mybir
from concourse.bass_utils import run_bass_kernel_spmd

F32 = mybir.dt.float32
BF16 = mybir.dt.bfloat16
AF = mybir.ActivationFunctionType
ALU = mybir.AluOpType
AX = mybir.AxisListType

NCORES = 8
D = 2048
L = 8192
TL = L // NCORES
DFF = 5632
NJ = DFF // 128
ALPHA = 2.0 ** 0.25
LN_EPS = 1e-5
LAM_INIT = 0.8 - 0.6 * math.exp(0.0)
NFFT = 2 * L
CG = 32

DEBUG = False


class TK:
    def __init__(self, nc, es, n_dma_sems=40):
        self.nc = nc
        self.engs = {'pe': nc.tensor, 'dve': nc.vector, 'act': nc.scalar,
                     'pool': nc.gpsimd, 'sp': nc.sync}
        self.sem = {}
        for k in self.engs:
            self.sem[k] = es.enter_context(nc.semaphore("sem_" + k))
        self.cnt = {k: 0 for k in self.engs}
        self.known = {k: {} for k in self.engs}
        self.res = {}
        self.dsem = [es.enter_context(nc.semaphore("dsem%d" % i)) for i in range(n_dma_sems)]
        self.dcnt = [0] * n_dma_sems
        self.dnext = 0
        self.flip = 0

    def _get(self, key):
        r = self.res.get(key)
        if r is None:
            r = {'w': None, 'r': {}}
            self.res[key] = r
        return r

    def _sem(self, sk):
        return self.sem[sk] if isinstance(sk, str) else self.dsem[sk]

    def _wait(self, eng, evs):
        e = self.engs[eng]
        kn = self.known[eng]
        for (sk, val, _) in evs:
            if kn.get(sk, 0) >= val:
                continue
            e.wait_ge(self._sem(sk), val)
            kn[sk] = val

    def op(self, eng, fn, reads=(), writes=(), dma=False):
        need = []
        for key in reads:
            w = self._get(key)['w']
            if w is None:
                continue
            if (not dma) and w[2] == eng and eng == 'pe':
                continue
            need.append(w)
        for key in writes:
            r = self._get(key)
            w = r['w']
            if w is not None and (dma or w[2] != eng):
                need.append(w)
            for sk, (val, weng) in r['r'].items():
                if dma or weng != eng:
                    need.append((sk, val, weng))
        if dma:
            d = self.dnext
            self.dnext = (self.dnext + 1) % len(self.dsem)
            if self.dcnt[d] > 0:
                need.append((d, self.dcnt[d], 'dma'))
        self._wait(eng, need)
        ins = fn(self.engs[eng])
        if dma:
            self.dcnt[d] += 16
            ins.then_inc(self.dsem[d], 16)
            ev = (d, self.dcnt[d], 'dma')
        else:
            self.cnt[eng] += 1
            ins.then_inc(self.sem[eng], 1)
            ev = (eng, self.cnt[eng], eng)
        for key in writes:
            self.res[key] = {'w': ev, 'r': {}}
        for key in reads:
            rr = self._get(key)['r']
            old = rr.get(ev[0])
            if old is None or old[0] < ev[1]:
                rr[ev[0]] = (ev[1], ev[2])
        return ins

    def dma(self, q, out, in_, reads=(), writes=()):
        return self.op(q, lambda e: e.dma_start(out=out, in_=in_), reads, writes, dma=True)

    def alt(self):
        self.flip ^= 1
        return 'act' if self.flip else 'dve'

    def copy(self, eng, out, in_, reads=(), writes=()):
        if eng == 'act':
            return self.op('act', lambda e: e.copy(out=out, in_=in_), reads, writes)
        return self.op(eng, lambda e: e.tensor_copy(out=out, in_=in_), reads, writes)

    def barrier(self):
        evs = [(k, self.cnt[k], k) for k in self.engs if self.cnt[k] > 0]
        evs += [(d, c, 'dma') for d, c in enumerate(self.dcnt) if c > 0]
        for k in self.engs:
            self._wait(k, [e for e in evs if e[0] != k])
        self.res = {}


def bcast_row(ap1d, parts, n):
    return ap1d.rearrange("(o n) -> o n", o=1).broadcast(0, parts) if hasattr(ap1d, "broadcast") else None


def ffn_ln(nc, tk, tag, x_dram, Wg, Wu, Wd, g_dram, b_dram, z_dram, out_tok, outT, identf, identb):
    with ExitStack() as es:
        sb = lambda name, shape, dt: es.enter_context(nc.sbuf_tensor(name + tag, shape, dt))
        xT = sb("xT", [128, 16, TL], BF16)
        h1T = sb("h1T", [128, NJ, TL], BF16)
        ps = [es.enter_context(nc.psum_tensor("ps%s%d" % (tag, i), [128, 512], F32)) for i in range(8)]
        xin = [sb("xin%d" % i, [128, 2048], F32) for i in range(2)]
        wg = [sb("wg%d" % i, [128, 16, 256], BF16) for i in range(2)]
        wu = [sb("wu%d" % i, [128, 16, 256], BF16) for i in range(2)]
        wd = [sb("wd%d" % i, [128, 4, 512], BF16) for i in range(3)]
        sg = [sb("sg%d" % i, [128, 512], F32) for i in range(2)]
        for t in range(8):
            buf = xin[t % 2]
            tk.dma('sp', buf[:], x_dram[t * 128:(t + 1) * 128, :], writes=[('xin', t % 2)])
            for kg in range(4):
                bi = (t * 4 + kg) % 8
                bank = ps[bi]
                for kk in range(4):
                    k = kg * 4 + kk
                    tk.op('pe', lambda e, kk=kk, k=k, bank=bank, buf=buf: e.transpose(
                        out=bank[:, kk * 128:(kk + 1) * 128], in_=buf[:, k * 128:(k + 1) * 128],
                        identity=identf[:]), reads=[('xin', t % 2), 'ident'], writes=[('ps', bi)])
                tk.copy(tk.alt(), xT[:, kg * 4:(kg + 1) * 4, t * 128:(t + 1) * 128],
                        bank[:].rearrange("p (a b) -> p a b", a=4),
                        reads=[('ps', bi)], writes=[('xT', t)])
        pi = 0
        for c in range(NJ // 2):
            s = c % 2
            tk.dma('pool', wg[s][:], Wg[:, c * 256:(c + 1) * 256].rearrange("(k p) n -> p k n", p=128),
                   writes=[('wg', s)])
            tk.dma('pool', wu[s][:], Wu[:, c * 256:(c + 1) * 256].rearrange("(k p) n -> p k n", p=128),
                   writes=[('wu', s)])
            for sub in range(2):
                j = c * 2 + sub
                for hh in range(2):
                    bg, bu = (pi * 2) % 8, (pi * 2 + 1) % 8
                    pi += 1
                    xr = [('xT', t) for t in range(hh * 4, hh * 4 + 4)]
                    for k in range(16):
                        tk.op('pe', lambda e, k=k, bg=bg, s=s, sub=sub, hh=hh: e.matmul(
                            ps[bg][:], wg[s][:, k, sub * 128:(sub + 1) * 128],
                            xT[:, k, hh * 512:(hh + 1) * 512], start=(k == 0), stop=(k == 15)),
                            reads=[('wg', s)] + xr, writes=[('ps', bg)])
                    for k in range(16):
                        tk.op('pe', lambda e, k=k, bu=bu, s=s, sub=sub, hh=hh: e.matmul(
                            ps[bu][:], wu[s][:, k, sub * 128:(sub + 1) * 128],
                            xT[:, k, hh * 512:(hh + 1) * 512], start=(k == 0), stop=(k == 15)),
                            reads=[('wu', s)] + xr, writes=[('ps', bu)])
                    si = pi % 2
                    tk.op('act', lambda e, bg=bg, si=si: e.activation(out=sg[si][:], in_=ps[bg][:], func=AF.Silu),
                          reads=[('ps', bg)], writes=[('sg', si)])
                    tk.op('dve', lambda e, bu=bu, si=si, j=j, hh=hh: e.tensor_tensor(
                        out=h1T[:, j, hh * 512:(hh + 1) * 512], in0=sg[si][:], in1=ps[bu][:], op=ALU.mult),
                        reads=[('sg', si), ('ps', bu)], writes=[('h1T', j, hh)])
        xres = [sb("xres%d" % i, [128, 512], F32) for i in range(2)]
        zt = [sb("zt%d" % i, [128, 512], F32) for i in range(2)]
        wi = 0
        for c in range(4):
            for jg in range(NJ // 4):
                s = wi % 3
                wi += 1
                tk.dma('pool', wd[s][:], Wd[jg * 512:(jg + 1) * 512, c * 512:(c + 1) * 512].rearrange(
                    "(j p) n -> p j n", p=128), writes=[('wd', s)])
                for jj in range(4):
                    j = jg * 4 + jj
                    for t in range(8):
                        tk.op('pe', lambda e, j=j, jj=jj, t=t, s=s: e.matmul(
                            ps[t][:], h1T[:, j, t * 128:(t + 1) * 128], wd[s][:, jj, :],
                            start=(j == 0), stop=(j == NJ - 1)),
                            reads=[('wd', s), ('h1T', j, t // 4)], writes=[('ps', t)])
            for t in range(8):
                r = t % 2
                tk.dma('sp', xres[r][:], x_dram[t * 128:(t + 1) * 128, c * 512:(c + 1) * 512],
                       writes=[('xres', r)])
                tk.op('act', lambda e, r=r: e.mul(out=xres[r][:], in_=xres[r][:], mul=ALPHA) if False else
                      e.activation(out=xres[r][:], in_=xres[r][:], func=AF.Copy, scale=ALPHA),
                      reads=[('xres', r)], writes=[('xres', r)])
                tk.op('dve', lambda e, r=r, t=t: e.scalar_tensor_tensor(
                    out=zt[r][:], in0=ps[t][:], scalar=0.5, in1=xres[r][:], op0=ALU.mult, op1=ALU.add),
                    reads=[('ps', t), ('xres', r)], writes=[('zt', r)])
                tk.dma('sp', z_dram[t * 128:(t + 1) * 128, c * 512:(c + 1) * 512], zt[r][:],
                       reads=[('zt', r)], writes=[('zd', t)])
    tk.barrier()
    with ExitStack() as es:
        ps = [es.enter_context(nc.psum_tensor("psl%s%d" % (tag, i), [128, 512], F32)) for i in range(4)]
        layer_norm_rows(nc, tk, es, tag, z_dram, g_dram, b_dram, out_tok, outT, ps, identb, zkeys=False)
    tk.barrier()


def layer_norm_rows(nc, tk, es, tag, z_dram, g_dram, b_dram, out_tok, outT, ps, identb, zkeys=True):
    sb = lambda name, shape, dt: es.enter_context(nc.sbuf_tensor(name + tag + "ln", shape, dt))
    gt = sb("gt", [128, 2048], F32)
    bt = sb("bt", [128, 2048], F32)
    tk.dma('sp', gt[:], g_dram.partition_broadcast(128), writes=['gt'])
    tk.dma('sp', bt[:], b_dram.partition_broadcast(128), writes=['bt'])
    zin = [sb("zin%d" % i, [128, 2048], F32) for i in range(2)]
    st = sb("st", [128, 4, 6], F32)
    mv = sb("mv", [128, 2], F32)
    rstd = sb("rstd", [128, 1], F32)
    ob = [sb("ob%d" % i, [128, 2048], BF16) for i in range(2)]
    oT = sb("oT", [128, 16, 128], BF16) if outT is not None else None
    for t in range(8):
        r = t % 2
        z = zin[r]
        tk.dma('sp', z[:], z_dram[t * 128:(t + 1) * 128, :], reads=[('zd', t)], writes=[('zin', r)])
        for q in range(4):
            tk.op('dve', lambda e, q=q, z=z: e.bn_stats(out=st[:, q, :], in_=z[:, q * 512:(q + 1) * 512]),
                  reads=[('zin', r)], writes=[('st', q)])
        tk.op('dve', lambda e: e.bn_aggr(out=mv[:], in_=st[:].rearrange("p a b -> p (a b)")),
              reads=[('st', q) for q in range(4)], writes=['mv'])
        tk.op('act', lambda e: e.activation(out=rstd[:], in_=mv[:, 1:2], func=AF.Sqrt, bias=LN_EPS, scale=1.0),
              reads=['mv'], writes=['rstd'])
        tk.op('dve', lambda e: e.reciprocal(out=rstd[:], in_=rstd[:]), reads=['rstd'], writes=['rstd'])
        tk.op('dve', lambda e, z=z: e.tensor_scalar(out=z[:], in0=z[:], scalar1=mv[:, 0:1], scalar2=rstd[:, 0:1],
                                                    op0=ALU.subtract, op1=ALU.mult),
              reads=[('zin', r), 'mv', 'rstd'], writes=[('zin', r)])
        tk.op('pool', lambda e, z=z: e.tensor_tensor(out=z[:], in0=z[:], in1=gt[:], op=ALU.mult),
              reads=[('zin', r), 'gt'], writes=[('zin', r)])
        tk.op('pool', lambda e, z=z: e.tensor_tensor(out=z[:], in0=z[:], in1=bt[:], op=ALU.add),
              reads=[('zin', r), 'bt'], writes=[('zin', r)])
        tk.dma('sp', out_tok[t * 128:(t + 1) * 128, :], z[:], reads=[('zin', r)], writes=[('otok', t)])
        if outT is not None:
            tk.copy('act', ob[r][:], z[:], reads=[('zin', r)], writes=[('ob', r)])
            for half in range(2):
                bi = (t % 2) * 2 + half
                pbv = ps[bi][:].bitcast(BF16)
                for kk in range(8):
                    k = half * 8 + kk
                    tk.op('pe', lambda e: e.transpose(out=pbv[:, kk * 128:(kk + 1) * 128],
                                                      in_=ob[r][:, k * 128:(k + 1) * 128], identity=identb[:]),
                          reads=[('ob', r), 'ident'], writes=[('ps', bi)])
                tk.copy('dve', oT[:, half * 8:(half + 1) * 8, :], pbv.rearrange("p (k n) -> p k n", k=8),
                        reads=[('ps', bi)], writes=[('oT', half)])
            tk.dma('sp', outT[:, t * 128:(t + 1) * 128].rearrange("(k p) n -> p k n", p=128), oT[:],
                   reads=[('oT', 0), ('oT', 1)], writes=[('outT', t)])


def mixer(nc, tk, A, identb):
    with ExitStack() as es:
        sb = lambda name, shape, dt: es.enter_context(nc.sbuf_tensor("m_" + name, shape, dt))
        ps = [es.enter_context(nc.psum_tensor("mps%d" % i, [128, 512], F32)) for i in range(8)]
        Wsl = sb("Wsl", [128, 16, 1024], BF16)
        for i in range(4):
            tk.dma('pool', Wsl[:, :, i * 256:(i + 1) * 256],
                   A['w_in_c'][:, i * 256:(i + 1) * 256].rearrange("(k p) n -> p k n", p=128), writes=[('Wsl', i)])
        WR = [('Wsl', i) for i in range(4)]
        qT = sb("qT", [128, L], BF16)
        kT = sb("kT", [128, L], BF16)
        Vaug = sb("Vaug", [128, 64, 130], BF16)
        tk.op('pool', lambda e: e.memset(Vaug[:, :, 128:130], 1.0), writes=['Vones'])
        hTb = [sb("hTb%d" % i, [128, 16, 512], BF16) for i in range(2)]
        cosb = [sb("cosb%d" % i, [128, 512], F32) for i in range(2)]
        sinb = [sb("sinb%d" % i, [128, 512], F32) for i in range(2)]
        stg = [sb("stg%d" % i, [128, 3, 512], F32) for i in range(2)]
        t1 = [sb("t1_%d" % i, [128, 512], F32) for i in range(2)]
        t2 = [sb("t2_%d" % i, [128, 512], F32) for i in range(2)]
        lq = sb("lq", [128, 4, 64], F32)
        for i, nm in enumerate(['lq1', 'lk1', 'lq2', 'lk2']):
            tk.dma('sp', lq[:, i, :], A[nm].partition_broadcast(128), writes=[('lq', i)])
        lp = sb("lp", [128, 2], F32)
        ljunk = sb("ljunk", [128, 64], F32)
        for i in range(2):
            tk.op('dve', lambda e, i=i: e.scalar_tensor_tensor(
                out=ljunk[:], in0=lq[:, 2 * i, :], scalar=1.0, in1=lq[:, 2 * i + 1, :],
                op0=ALU.mult, op1=ALU.mult, accum_out=lp[:, i:i + 1]),
                reads=[('lq', 2 * i), ('lq', 2 * i + 1)], writes=[('lp', i), 'ljunk'])
        le = sb("le", [128, 2], F32)
        tk.op('act', lambda e: e.activation(out=le[:], in_=lp[:], func=AF.Exp),
              reads=[('lp', 0), ('lp', 1)], writes=['le'])
        neglam = sb("neglam", [128, 1], F32)
        tk.op('dve', lambda e: e.scalar_tensor_tensor(out=neglam[:], in0=le[:, 1:2], scalar=-LAM_INIT,
                                                      in1=le[:, 0:1], op0=ALU.add, op1=ALU.subtract),
              reads=['le'], writes=['neglam'])
        gsub = sb("gsub", [128, 128], F32)
        tk.dma('sp', gsub[:], A['subln_g'].partition_broadcast(128), writes=['gsub'])
        tk.op('dve', lambda e: e.tensor_scalar(out=gsub[:], in0=gsub[:], scalar1=(1.0 - LAM_INIT), scalar2=None,
                                               op0=ALU.mult), reads=['gsub'], writes=['gsub'])
        pi = 0
        for blk in range(16):
            s = blk % 2
            tk.dma('sp', hTb[s][:], A['h_allT'][(blk // 2) * D:(blk // 2 + 1) * D,
                                                (blk % 2) * 512:(blk % 2 + 1) * 512].rearrange(
                "(k p) n -> p k n", p=128), reads=['h_allT'], writes=[('hTb', s)])
            tk.dma('sp', cosb[s][:], A['cos_tab'][:, blk * 512:(blk + 1) * 512], writes=[('cosb', s)])
            tk.dma('sp', sinb[s][:], A['sin_tab'][:, blk * 512:(blk + 1) * 512], writes=[('sinb', s)])

            def proj(col0, bank):
                for k in range(16):
                    tk.op('pe', lambda e, k=k: e.matmul(ps[bank][:], Wsl[:, k, col0:col0 + 128], hTb[s][:, k, :],
                                                       start=(k == 0), stop=(k == 15)),
                          reads=WR + [('hTb', s)], writes=[('ps', bank)])
            for m in range(3):
                b = pi % 8
                pi += 1
                proj(m * 128, b)
                tk.copy(tk.alt(), stg[s][:, m, :], ps[b][:], reads=[('ps', b)], writes=[('stg', s, m)])
            tk.dma('sp', A['phy'].rearrange("m c t -> c m t")[:, :, blk * 512:(blk + 1) * 512], stg[s][:],
                   reads=[('stg', s, m) for m in range(3)], writes=['phy'])
            for (dst, c0, cp, nm) in ((qT, 384, 768, 'qT'), (kT, 512, 896, 'kT')):
                ba, bb = pi % 8, (pi + 1) % 8
                pi += 2
                proj(c0, ba)
                proj(cp, bb)
                tk.op('dve', lambda e: e.tensor_tensor(out=t1[s][:], in0=ps[ba][:], in1=cosb[s][:], op=ALU.mult),
                      reads=[('ps', ba), ('cosb', s)], writes=[('t1', s)])
                tk.op('dve', lambda e: e.tensor_tensor(out=t2[s][:], in0=ps[bb][:], in1=sinb[s][:], op=ALU.mult),
                      reads=[('ps', bb), ('sinb', s)], writes=[('t2', s)])
                tk.op('pool', lambda e, dst=dst: e.tensor_tensor(out=dst[:, blk * 512:(blk + 1) * 512], in0=t1[s][:],
                                                                 in1=t2[s][:], op=ALU.add),
                      reads=[('t1', s), ('t2', s)], writes=[(nm, blk)])
            b = pi % 8
            pi += 1
            for tt in range(4):
                for k in range(16):
                    tk.op('pe', lambda e, k=k, tt=tt: e.matmul(
                        ps[b][:, tt * 128:(tt + 1) * 128], hTb[s][:, k, tt * 128:(tt + 1) * 128], Wsl[:, k, 640:768],
                        start=(k == 0 and tt == 0), stop=(k == 15), skip_group_check=True),
                        reads=WR + [('hTb', s)], writes=[('ps', b)])
            tk.copy('act', Vaug[:, blk * 4:(blk + 1) * 4, 0:128], ps[b][:].rearrange("p (a b) -> p a b", a=4),
                    reads=[('ps', b)], writes=[('V', blk)])
        E = [sb("E%d" % i, [128, 2, 512], BF16) for i in range(2)]
        rc = sb("rc", [128, 2], F32)
        c2 = sb("c2", [128, 1], F32)
        o1 = sb("o1", [128, 128], F32)
        o2 = sb("o2", [128, 128], F32)
        ojunk = sb("ojunk", [128, 128], F32)
        ssq = sb("ssq", [128, 1], F32)
        rstd = sb("rstda", [128, 1], F32)
        ob = [sb("oba%d" % i, [128, 128], BF16) for i in range(2)]
        VR = [('V', b) for b in range(16)] + ['Vones']
        oi = 0
        for Q in range(16):
            for kt in range(64):
                sbi = kt % 2
                for m in range(2):
                    tk.op('pe', lambda e, m=m: e.matmul(
                        ps[sbi * 2 + m][:], kT[m * 64:(m + 1) * 64, kt * 128:(kt + 1) * 128],
                        qT[m * 64:(m + 1) * 64, Q * 512:(Q + 1) * 512], start=True, stop=True),
                        reads=[('kT', kt // 4), ('qT', Q)], writes=[('ps', sbi * 2 + m)])
                for m in range(2):
                    tk.op('act', lambda e, m=m: e.activation(out=E[sbi][:, m, :], in_=ps[sbi * 2 + m][:],
                                                             func=AF.Exp, scale=0.125),
                          reads=[('ps', sbi * 2 + m)], writes=[('E', sbi, m)])
                for qt in range(4):
                    for m in range(2):
                        tk.op('pe', lambda e, m=m, qt=qt: e.matmul(
                            ps[4 + qt][:, m * 160:m * 160 + 129], E[sbi][:, m, qt * 128:(qt + 1) * 128],
                            Vaug[:, kt, 0:129], start=(kt == 0 and m == 0), stop=(kt == 63), skip_group_check=True),
                            reads=[('E', sbi, m)] + (VR if kt == 0 else []), writes=[('ps', 4 + qt)])
            for qt in range(4):
                pb = ps[4 + qt]
                for m in range(2):
                    tk.op('dve', lambda e, m=m: e.reciprocal(out=rc[:, m:m + 1], in_=pb[:, m * 160 + 128:m * 160 + 129]),
                          reads=[('ps', 4 + qt)], writes=[('rc', m)])
                tk.op('dve', lambda e: e.tensor_tensor(out=c2[:], in0=rc[:, 1:2], in1=neglam[:], op=ALU.mult),
                      reads=[('rc', 1), 'neglam'], writes=['c2'])
                tk.op('dve', lambda e: e.tensor_scalar(out=o1[:], in0=pb[:, 0:128], scalar1=rc[:, 0:1], scalar2=None,
                                                       op0=ALU.mult), reads=[('ps', 4 + qt), ('rc', 0)], writes=['o1'])
                tk.op('dve', lambda e: e.scalar_tensor_tensor(out=o2[:], in0=pb[:, 160:288], scalar=c2[:, 0:1],
                                                              in1=o1[:], op0=ALU.mult, op1=ALU.add),
                      reads=[('ps', 4 + qt), 'c2', 'o1'], writes=['o2'])
                tk.op('dve', lambda e: e.scalar_tensor_tensor(out=ojunk[:], in0=o2[:], scalar=1.0, in1=o2[:],
                                                              op0=ALU.mult, op1=ALU.mult, accum_out=ssq[:]),
                      reads=['o2'], writes=['ojunk', 'ssq'])
                tk.op('act', lambda e: e.activation(out=rstd[:], in_=ssq[:], func=AF.Sqrt, bias=LN_EPS, scale=1.0 / 128),
                      reads=['ssq'], writes=['rstd'])
                tk.op('dve', lambda e: e.reciprocal(out=rstd[:], in_=rstd[:]), reads=['rstd'], writes=['rstd'])
                o = oi % 2
                oi += 1
                tk.op('dve', lambda e: e.scalar_tensor_tensor(out=ob[o][:], in0=o2[:], scalar=rstd[:, 0:1], in1=gsub[:],
                                                              op0=ALU.mult, op1=ALU.mult),
                      reads=['o2', 'rstd', 'gsub'], writes=[('oba', o)])
                r0 = Q * 512 + qt * 128
                tk.dma('sp', A['a2a_in'][r0:r0 + 128, 128:256], ob[o][:], reads=[('oba', o)], writes=['a2a_in'])
    tk.barrier()


def hyena(nc, tk, A, identb):
    PI = math.pi
    with ExitStack() as es:
        sb = lambda name, shape, dt: es.enter_context(nc.sbuf_tensor("h1_" + name, shape, dt))
        cw = sb("cw", [128, 9], F32)
        cb = sb("cb", [128, 3], F32)
        tk.dma('sp', cw[:], A['conv_w_c'], writes=['cw'])
        tk.dma('sp', cb[:], A['conv_b_c'], writes=['cb'])
        pin = sb("pin", [128, L + 2], F32)
        zc = sb("zc", [128, L], F32)
        ub = sb("ub", [128, L], BF16)
        tk.op('pool', lambda e: e.memset(pin[:, 0:1], 0.0), writes=['pin0'])
        tk.op('pool', lambda e: e.memset(pin[:, L + 1:L + 2], 0.0), writes=['pin1'])
        for m in range(3):
            tk.dma('sp', pin[:, 1:L + 1], A['phy'][m], reads=['phy'], writes=['pin'])
            R = ['pin', 'pin0', 'pin1', 'cw', 'cb']
            tk.op('dve', lambda e: e.tensor_scalar(out=zc[:], in0=pin[:, 1:L + 1], scalar1=cw[:, 3 + m:4 + m],
                                                   scalar2=cb[:, m:m + 1], op0=ALU.mult, op1=ALU.add),
                  reads=R, writes=['zc'])
            tk.op('dve', lambda e: e.scalar_tensor_tensor(out=zc[:], in0=pin[:, 0:L], scalar=cw[:, m:m + 1],
                                                          in1=zc[:], op0=ALU.mult, op1=ALU.add),
                  reads=R + ['zc'], writes=['zc'])
            tk.op('dve', lambda e: e.scalar_tensor_tensor(out=zc[:], in0=pin[:, 2:L + 2], scalar=cw[:, 6 + m:7 + m],
                                                          in1=zc[:], op0=ALU.mult, op1=ALU.add),
                  reads=R + ['zc'], writes=['zc'])
            if m == 0:
                tk.copy('act', ub[:], zc[:], reads=['zc'], writes=['ub'])
                tk.dma('sp', A['ud'], ub[:], reads=['ub'], writes=['ud'])
            tk.dma('sp', A['gd'][m], zc[:], reads=['zc'], writes=[('gd', m)])
    tk.barrier()
    with ExitStack() as es:
        sb = lambda name, shape, dt: es.enter_context(nc.sbuf_tensor("h2_" + name, shape, dt))
        ps = [es.enter_context(nc.psum_tensor("h2ps%d" % i, [128, 512], F32)) for i in range(8)]
        w1 = sb("w1", [33, 64], F32)
        w2 = sb("w2", [64, 64], F32)
        w3 = sb("w3", [64, 64], F32)
        fb = sb("fb", [64, 4], F32)
        mpi = sb("mpi", [64, 1], F32)
        tk.dma('sp', w1[:], A['fw1'], writes=['w1'])
        tk.dma('sp', w2[:], A['fw2'], writes=['w2'])
        tk.dma('sp', w3[:], A['fw3'], writes=['w3'])
        tk.dma('sp', fb[:], A['fbf'], writes=['fb'])
        tk.op('pool', lambda e: e.memset(mpi[:], -PI), writes=['mpi'])
        fwo = sb("fwo", [64, 512], BF16)
        tk.dma('pool', fwo[:], A['fw_out_c'], writes=['fwo'])
        h3k = sb("h3k", [64, NFFT], BF16)
        zkb = [sb("zkb%d" % i, [33, 512], F32) for i in range(2)]
        ta = [sb("ta%d" % i, [64, 512], F32) for i in range(2)]
        tm = [sb("tm%d" % i, [64, 512], F32) for i in range(2)]
        hh = [sb("hh%d" % i, [64, 512], F32) for i in range(2)]
        pi = 0
        for ch in range(NFFT // 512):
            s = ch % 2
            tk.dma('sp', zkb[s][:], A['zk'][:, ch * 512:(ch + 1) * 512], writes=[('zkb', s)])
            src, srckey = zkb[s], ('zkb', s)
            for li, (w, wk) in enumerate(((w1, 'w1'), (w2, 'w2'), (w3, 'w3'))):
                b = pi % 8
                pi += 1
                kk = 33 if li == 0 else 64
                tk.op('pe', lambda e: e.matmul(ps[b][0:64, :], w[0:kk, :], src[0:kk, :], start=True, stop=True),
                      reads=[wk, srckey], writes=[('ps', b)])
                tk.op('dve', lambda e: e.tensor_scalar(out=ta[s][:], in0=ps[b][0:64, :], scalar1=fb[:, li:li + 1],
                                                       scalar2=fb[:, 3:4], op0=ALU.add, op1=ALU.mult),
                      reads=[('ps', b), 'fb'], writes=[('ta', s)])
                tk.op('dve', lambda e: e.tensor_scalar(out=tm[s][:], in0=ta[s][:], scalar1=PI, scalar2=-2 * PI,
                                                       op0=ALU.is_gt, op1=ALU.mult), reads=[('ta', s)], writes=[('tm', s)])
                tk.op('dve', lambda e: e.tensor_tensor(out=ta[s][:], in0=ta[s][:], in1=tm[s][:], op=ALU.add),
                      reads=[('ta', s), ('tm', s)], writes=[('ta', s)])
                tk.op('dve', lambda e: e.tensor_scalar(out=tm[s][:], in0=ta[s][:], scalar1=-PI, scalar2=2 * PI,
                                                       op0=ALU.is_lt, op1=ALU.mult), reads=[('ta', s)], writes=[('tm', s)])
                tk.op('dve', lambda e: e.tensor_tensor(out=ta[s][:], in0=ta[s][:], in1=tm[s][:], op=ALU.add),
                      reads=[('ta', s), ('tm', s)], writes=[('ta', s)])
                if li < 2:
                    tk.op('act', lambda e: e.activation(out=hh[s][:], in_=ta[s][:], func=AF.Sin),
                          reads=[('ta', s)], writes=[('hh', s)])
                    src, srckey = hh[s], ('hh', s)
                else:
                    tk.op('act', lambda e: e.activation(out=h3k[:, ch * 512:(ch + 1) * 512], in_=ta[s][:],
                                                        func=AF.Sin),
                          reads=[('ta', s)], writes=[('h3k', ch)])
        dec = [sb("dec%d" % i, [128, 512], F32) for i in range(2)]
        kb = [sb("kb%d" % i, [128, 512], BF16) for i in range(2)]
        ki = 0
        for ch in range(NFFT // 512):
            s = ch % 2
            dr = ch // 16
            tk.dma('sp', dec[s][:], A['decay_k'][:, ch * 512:(ch + 1) * 512], writes=[('dec', s)])
            for n in range(2):
                b = pi % 8
                pi += 1
                tk.op('pe', lambda e: e.matmul(ps[b][:], fwo[:, (n * 2 + dr) * 128:(n * 2 + dr + 1) * 128],
                                               h3k[:, ch * 512:(ch + 1) * 512], start=True, stop=True),
                      reads=['fwo', ('h3k', ch)], writes=[('ps', b)])
                q = ki % 2
                ki += 1
                tk.op('dve', lambda e: e.tensor_tensor(out=kb[q][:], in0=ps[b][:], in1=dec[s][:], op=ALU.mult),
                      reads=[('ps', b), ('dec', s)], writes=[('kb', q)])
                tk.dma('sp', A['kd'][n][:, ch * 512:(ch + 1) * 512], kb[q][:], reads=[('kb', q)], writes=[('kd', n)])
    tk.barrier()
    with ExitStack() as es:
        sb = lambda name, shape, dt: es.enter_context(nc.sbuf_tensor("h3_" + name, shape, dt))
        ps = [es.enter_context(nc.psum_tensor("h3ps%d" % i, [128, 512], F32)) for i in range(8)]
        F1 = sb("F1", [128, 256], BF16)
        F2 = sb("F2", [128, 2, 256], BF16)
        H = sb("H", [128, 128, 2, 64], BF16)
        tk.dma('sp', F1[:], A['F1'], writes=['F1'])
        tk.dma('sp', F2[:], A['F2'], writes=['F2'])
        for i in range(4):
            tk.dma('sp', H[:, i * 32:(i + 1) * 32], A['H'][:, i * 32:(i + 1) * 32], writes=[('H', i)])
        HR = [('H', i) for i in range(4)]
        Y = sb("Y", [128, CG, 2, 128], BF16)
        Qs = sb("Qs", [128, 128, 2, CG], BF16)
        hb = sb("hb", [CG, 2], F32)
        pi = 0
        gi = 0
        for g in range(128 // CG):
            c0 = g * CG
            tk.dma('sp', hb[:], A['hbias_c'][c0:c0 + CG, :], writes=['hb'])
            for n in range(2):
              with ExitStack() as es2:
                sb2 = lambda name, shape, dt: es2.enter_context(nc.sbuf_tensor("h3a%d%d_" % (g, n) + name, shape, dt))
                Gb = [sb2("Gb%d" % i, [128, 8, 256], BF16) for i in range(3)]
                Du = sb2("Du", [64, CG, 128], BF16)
                Dk = sb2("Dk", [128, CG, 128], BF16)
                Au = sb2("Au", [128, CG, 384], BF16)
                Ak = sb2("Ak", [128, CG, 384], BF16)
                Kc = [sb2("Kc%d" % i, [128, 512], F32) for i in range(2)]
                P1 = [sb2("P1_%d" % i, [128, 512], F32) for i in range(2)]
                T3 = [sb2("T3_%d" % i, [128, 256], F32) for i in range(2)]
                T4 = [sb2("T4_%d" % i, [128, 256], F32) for i in range(2)]
                usrc = A['ud'] if n == 0 else A['y1d']
                tk.dma('sp', Du[:], usrc[c0:c0 + CG, :].rearrange("c (a b) -> a c b", b=128),
                       reads=['ud' if n == 0 else 'y1d'], writes=['Du'])
                tk.dma('sp', Dk[:], A['kd'][n][c0:c0 + CG, :].rearrange("c (a b) -> a c b", b=128),
                       reads=[('kd', n)], writes=['Dk'])
                for (Dt, At, KA, dk, ak) in ((Du, Au, 64, 'Du', 'Au'), (Dk, Ak, 128, 'Dk', 'Ak')):
                    for cp in range(CG // 2):
                        b = pi % 8
                        pi += 1
                        for i in range(2):
                            tk.op('pe', lambda e: e.matmul(ps[b][:, i * 256:(i + 1) * 256], Dt[0:KA, cp * 2 + i, :],
                                                           F1[0:KA, :], start=(i == 0), stop=True, skip_group_check=True),
                                  reads=[dk, 'F1'], writes=[('ps', b)])
                        pv = ps[b][:].rearrange("p (i x) -> p i x", i=2)
                        tk.copy('dve', At[:, cp * 2:cp * 2 + 2, 0:256], pv, reads=[('ps', b)], writes=[ak])
                        tk.op('act', lambda e: e.activation(out=At[:, cp * 2:cp * 2 + 2, 256:384], in_=pv[:, :, 128:256],
                                                            func=AF.Copy, scale=-1.0), reads=[('ps', b)], writes=[ak])
                for kc in range(16):
                    gs = gi % 3
                    gi += 1
                    tk.dma('sp', Gb[gs][:], A['G'][:, kc * 8:(kc + 1) * 8, :], writes=[('Gb', gs)])
                    banks = {}
                    for (At, ak) in ((Ak, 'Ak'), (Au, 'Au')):
                        b = pi % 8
                        pi += 1
                        banks[ak] = b
                        first = True
                        for i in range(8):
                            k1 = kc * 8 + i
                            Ar, Ai, nAi = At[:, :, k1], At[:, :, 128 + k1], At[:, :, 256 + k1]
                            Gr, Gi = Gb[gs][:, i, 0:128], Gb[gs][:, i, 128:256]
                            o_r = ps[b][:, i * 64:i * 64 + 32]
                            o_i = ps[b][:, i * 64 + 32:i * 64 + 64]
                            for (oo, lh, rh) in ((o_r, Gr, Ar), (o_r, Gi, nAi), (o_i, Gi, Ar), (o_i, Gr, Ai)):
                                tk.op('pe', lambda e: e.matmul(oo, lh, rh, start=first, stop=True, skip_group_check=True),
                                      reads=[('Gb', gs), ak], writes=[('ps', b)])
                                first = False
                    bk, bu = banks['Ak'], banks['Au']
                    q = kc % 2
                    tk.copy('act', Kc[q][:], ps[bk][:], reads=[('ps', bk)], writes=[('Kc', q)])
                    pu = ps[bu][:].rearrange("p (i r c) -> p i r c", i=8, r=2)
                    kv = Kc[q][:].rearrange("p (i r c) -> p i r c", i=8, r=2)
                    tk.op('dve', lambda e: e.tensor_tensor(out=P1[q][:], in0=ps[bu][:], in1=Kc[q][:], op=ALU.mult),
                          reads=[('ps', bu), ('Kc', q)], writes=[('P1', q)])
                    t3v = T3[q][:].rearrange("p (i c) -> p i c", i=8)
                    t4v = T4[q][:].rearrange("p (i c) -> p i c", i=8)
                    tk.op('dve', lambda e: e.tensor_tensor(out=t3v, in0=pu[:, :, 0, :], in1=kv[:, :, 1, :], op=ALU.mult),
                          reads=[('ps', bu), ('Kc', q)], writes=[('T3', q)])
                    tk.op('dve', lambda e: e.tensor_tensor(out=t4v, in0=pu[:, :, 1, :], in1=kv[:, :, 0, :], op=ALU.mult),
                          reads=[('ps', bu), ('Kc', q)], writes=[('T4', q)])
                    p1v = P1[q][:].rearrange("p (i r c) -> p c r i", i=8, r=2)
                    tk.op('pool', lambda e: e.tensor_tensor(out=Y[:, :, 0, kc * 8:(kc + 1) * 8], in0=p1v[:, :, 0, :],
                                                            in1=p1v[:, :, 1, :], op=ALU.subtract),
                          reads=[('P1', q)], writes=['Y'])
                    tk.op('pool', lambda e: e.tensor_tensor(out=Y[:, :, 1, kc * 8:(kc + 1) * 8],
                                                            in0=T3[q][:].rearrange("p (i c) -> p c i", i=8),
                                                            in1=T4[q][:].rearrange("p (i c) -> p c i", i=8), op=ALU.add),
                          reads=[('T3', q), ('T4', q)], writes=['Y'])
              tk.barrier()
              with ExitStack() as es2:
                sb2 = lambda name, shape, dt: es2.enter_context(nc.sbuf_tensor("h3b%d%d_" % (g, n) + name, shape, dt))
                yc = sb2("yc", [CG, L], F32)
                ufc = [sb2("ufc%d" % i, [CG, 2048], F32) for i in range(2)]
                gtc = [sb2("gtc%d" % i, [CG, 2048], F32) for i in range(2)]
                ybc = [sb2("ybc%d" % i, [CG, 2048], BF16) for i in range(2)]
                for cp in range(CG // 2):
                    b = pi % 8
                    pi += 1
                    for i in range(2):
                        c = cp * 2 + i
                        tk.op('pe', lambda e: e.matmul(ps[b][:, i * 256:(i + 1) * 256], Y[:, c, 0, :], F2[:, 0, :],
                                                       start=(i == 0), stop=False, skip_group_check=True),
                              reads=['Y', 'F2'], writes=[('ps', b)])
                        tk.op('pe', lambda e: e.matmul(ps[b][:, i * 256:(i + 1) * 256], Y[:, c, 1, :], F2[:, 1, :],
                                                       start=False, stop=True, skip_group_check=True),
                              reads=['Y', 'F2'], writes=[('ps', b)])
                    tk.copy(tk.alt(), Qs[:, :, :, cp * 2:cp * 2 + 2].rearrange("p b r c -> p c r b"),
                            ps[b][:].rearrange("p (c r b) -> p c r b", c=2, r=2), reads=[('ps', b)], writes=['Qs'])
                ycv = yc[:].rearrange("c (a b) -> c b a", b=128)
                for bc in range(16):
                    b = pi % 8
                    pi += 1
                    for i in range(8):
                        bb = bc * 8 + i
                        tk.op('pe', lambda e: e.matmul(ps[b][0:CG, i * 64:(i + 1) * 64], Qs[:, bb, 0, :], H[:, bb, 0, :],
                                                       start=(i == 0), stop=False, skip_group_check=True),
                              reads=['Qs'] + HR, writes=[('ps', b)])
                        tk.op('pe', lambda e: e.matmul(ps[b][0:CG, i * 64:(i + 1) * 64], Qs[:, bb, 1, :], H[:, bb, 1, :],
                                                       start=False, stop=True, skip_group_check=True),
                              reads=['Qs'] + HR, writes=[('ps', b)])
                    tk.op('act', lambda e: e.activation(out=ycv[:, bc * 8:(bc + 1) * 8, :],
                                                        in_=ps[b][0:CG, :].rearrange("c (i a) -> c i a", i=8),
                                                        func=AF.Copy, scale=1.0 / NFFT),
                          reads=[('ps', b)], writes=['yc'])
                for q4 in range(4):
                    s2 = q4 % 2
                    cs = slice(q4 * 2048, (q4 + 1) * 2048)
                    usrc32 = A['gd'][0] if n == 0 else A['y1f']
                    tk.dma('sp', ufc[s2][:], usrc32[c0:c0 + CG, cs], reads=[('gd', 0), 'y1f'], writes=[('ufc', s2)])
                    tk.dma('sp', gtc[s2][:], A['gd'][1 + n][c0:c0 + CG, cs], reads=[('gd', 1 + n)], writes=[('gtc', s2)])
                    tk.op('dve', lambda e: e.scalar_tensor_tensor(out=ufc[s2][:], in0=ufc[s2][:], scalar=hb[:, n:n + 1],
                                                                  in1=yc[:, cs], op0=ALU.mult, op1=ALU.add),
                          reads=[('ufc', s2), 'hb', 'yc'], writes=[('ufc', s2)])
                    if n == 0:
                        tk.op('dve', lambda e: e.tensor_tensor(out=ufc[s2][:], in0=ufc[s2][:], in1=gtc[s2][:], op=ALU.mult),
                              reads=[('ufc', s2), ('gtc', s2)], writes=[('ufc', s2)])
                        tk.copy('act', ybc[s2][:], ufc[s2][:], reads=[('ufc', s2)], writes=[('ybc', s2)])
                        tk.dma('sp', A['y1f'][c0:c0 + CG, cs], ufc[s2][:], reads=[('ufc', s2)], writes=['y1f'])
                        tk.dma('sp', A['y1d'][c0:c0 + CG, cs], ybc[s2][:], reads=[('ybc', s2)], writes=['y1d'])
                    else:
                        tk.op('dve', lambda e: e.tensor_tensor(out=ybc[s2][:], in0=ufc[s2][:], in1=gtc[s2][:], op=ALU.mult),
                              reads=[('ufc', s2), ('gtc', s2)], writes=[('ybc', s2)])
                        tk.dma('sp', A['y2d'][c0:c0 + CG, cs], ybc[s2][:], reads=[('ybc', s2)], writes=['y2d'])
              tk.barrier()
    tk.barrier()
    with ExitStack() as es:
        sb = lambda name, shape, dt: es.enter_context(nc.sbuf_tensor("h4_" + name, shape, dt))
        pst = [es.enter_context(nc.psum_tensor("h4ps%d" % i, [128, 1024], BF16)) for i in range(2)]
        y2 = sb("y2", [128, L], BF16)
        tk.dma('sp', y2[:], A['y2d'], reads=['y2d'], writes=['y2'])
        ot = [sb("ot%d" % i, [128, 8, 128], BF16) for i in range(2)]
        for tg in range(8):
            s = tg % 2
            for i in range(8):
                t = tg * 8 + i
                tk.op('pe', lambda e: e.transpose(out=pst[s][:, i * 128:(i + 1) * 128], in_=y2[:, t * 128:(t + 1) * 128],
                                                  identity=identb[:]), reads=['y2', 'ident'], writes=[('pst', s)])
            tk.copy(tk.alt(), ot[s][:], pst[s][:].rearrange("p (i c) -> p i c", i=8), reads=[('pst', s)],
                    writes=[('ot', s)])
            tk.dma('sp', A['a2a_in'][tg * 1024:(tg + 1) * 1024, 0:128].rearrange("(i p) c -> p i c", p=128), ot[s][:],
                   reads=[('ot', s)], writes=['a2a_in'])
    tk.barrier()


def phase_c(nc, tk, A, identb):
    with ExitStack() as es:
        sb = lambda name, shape, dt: es.enter_context(nc.sbuf_tensor("c_" + name, shape, dt))
        ps = [es.enter_context(nc.psum_tensor("cps%d" % i, [128, 512], F32)) for i in range(8)]
        Wo = sb("Wo", [128, 16, 2048], BF16)
        for i in range(4):
            tk.dma('pool', Wo[:, :, i * 512:(i + 1) * 512],
                   A['w_out'][:, i * 512:(i + 1) * 512].rearrange("(k p) n -> p k n", p=128), writes=[('Wo', i)])
        WR = [('Wo', i) for i in range(4)]
        hg = sb("hg", [128, 1024], F32)
        tk.dma('sp', hg[:], A['hyena_norm_g'].partition_broadcast(128), writes=['hg'])
        mx = [sb("mx%d" % i, [128, 8, 256], BF16) for i in range(2)]
        hy = sb("hy", [128, 8, 128], F32)
        hjunk = sb("hjunk", [128, 1024], F32)
        ss = sb("ss", [128, 1], F32)
        rstd = sb("rstd", [128, 1], F32)
        cat = [sb("cat%d" % i, [128, 2048], BF16) for i in range(2)]
        catT = [sb("catT%d" % i, [128, 16, 128], BF16) for i in range(2)]
        hres = [sb("hres%d" % i, [128, 2048], F32) for i in range(2)]
        zt = [sb("zt%d" % i, [128, 2048], F32) for i in range(2)]
        a2v = A['a2a_out'].rearrange("(i t) c -> t i c", i=8)
        cand = [sb("cand%d" % i, [128, 8, 256], BF16) for i in range(3)]
        sel = sb("sel", [128, 8], F32)
        tk.dma('sp', sel[:], A['sel'], writes=['sel'])
        for t in range(8):
            s = t % 2
            for jj in range(8):
                cs_ = jj % 3
                tk.dma('sp', cand[cs_][:], a2v[jj * TL + t * 128:jj * TL + (t + 1) * 128], reads=['a2a_out'],
                       writes=[('cand', cs_)])
                if jj == 0:
                    tk.op('dve', lambda e: e.tensor_scalar(out=mx[s][:], in0=cand[cs_][:], scalar1=sel[:, 0:1],
                                                           scalar2=None, op0=ALU.mult),
                          reads=[('cand', cs_), 'sel'], writes=[('mx', s)])
                else:
                    tk.op('dve', lambda e: e.scalar_tensor_tensor(out=mx[s][:], in0=cand[cs_][:], scalar=sel[:, jj:jj + 1],
                                                                  in1=mx[s][:], op0=ALU.mult, op1=ALU.add),
                          reads=[('cand', cs_), 'sel', ('mx', s)], writes=[('mx', s)])
            tk.dma('sp', hres[s][:], A['h_tok'][t * 128:(t + 1) * 128, :], reads=['h_tok'], writes=[('hres', s)])
            tk.copy('dve', hy[:], mx[s][:, :, 0:128], reads=[('mx', s)], writes=['hy'])
            hyf = hy[:].rearrange("p a b -> p (a b)")
            tk.op('dve', lambda e: e.scalar_tensor_tensor(out=hjunk[:], in0=hyf, scalar=1.0, in1=hyf,
                                                          op0=ALU.mult, op1=ALU.mult, accum_out=ss[:]),
                  reads=['hy'], writes=['hjunk', 'ss'])
            tk.op('act', lambda e: e.activation(out=rstd[:], in_=ss[:], func=AF.Sqrt, bias=LN_EPS, scale=1.0 / 1024),
                  reads=['ss'], writes=['rstd'])
            tk.op('dve', lambda e: e.reciprocal(out=rstd[:], in_=rstd[:]), reads=['rstd'], writes=['rstd'])
            tk.op('dve', lambda e: e.scalar_tensor_tensor(out=cat[s][:, 0:1024], in0=hyf, scalar=rstd[:, 0:1], in1=hg[:],
                                                          op0=ALU.mult, op1=ALU.mult),
                  reads=['hy', 'rstd', 'hg'], writes=[('cat', s, 0)])
            tk.copy('act', cat[s][:, 1024:2048].rearrange("p (a b) -> p a b", a=8), mx[s][:, :, 128:256],
                    reads=[('mx', s)], writes=[('cat', s, 1)])
            for half in range(2):
                bi = s * 2 + half
                pv = ps[bi][:].bitcast(BF16)
                for kk in range(8):
                    k = half * 8 + kk
                    tk.op('pe', lambda e: e.transpose(out=pv[:, kk * 128:(kk + 1) * 128],
                                                      in_=cat[s][:, k * 128:(k + 1) * 128], identity=identb[:]),
                          reads=[('cat', s, 0), ('cat', s, 1), 'ident'], writes=[('ps', bi)])
                tk.copy(tk.alt(), catT[s][:, half * 8:(half + 1) * 8, :], pv.rearrange("p (k n) -> p k n", k=8),
                        reads=[('ps', bi)], writes=[('catT', s, half)])
            tk.op('act', lambda e: e.activation(out=hres[s][:], in_=hres[s][:], func=AF.Copy, scale=ALPHA),
                  reads=[('hres', s)], writes=[('hres', s)])
            for c in range(4):
                for k in range(16):
                    tk.op('pe', lambda e: e.matmul(ps[4 + c][:], catT[s][:, k, :], Wo[:, k, c * 512:(c + 1) * 512],
                                                   start=(k == 0), stop=(k == 15)),
                          reads=WR + [('catT', s, 0), ('catT', s, 1)], writes=[('ps', 4 + c)])
                tk.op('dve', lambda e: e.tensor_tensor(out=zt[s][:, c * 512:(c + 1) * 512], in0=ps[4 + c][:],
                                                       in1=hres[s][:, c * 512:(c + 1) * 512], op=ALU.add),
                      reads=[('ps', 4 + c), ('hres', s)], writes=[('zt', s, c)])
            tk.dma('sp', A['z_dram'][t * 128:(t + 1) * 128, :], zt[s][:], reads=[('zt', s, c) for c in range(4)],
                   writes=[('zd', t)])
        layer_norm_rows(nc, tk, es, "c", A['z_dram'], A['ln2_g'], A['ln2_b'], A['x2_tok'], None, ps, identb)
    tk.barrier()


def build():
    nc = bass.Bass("TRN2", target_bir_lowering=False)
    A = {}

    def inp(name, shape, dt=F32):
        if TEST_MIXER and (name.startswith('ffn') or name in ('x', 'ln1_g', 'ln1_b', 'ln3_g', 'ln3_b')):
            return
        A[name] = nc.dram_tensor(name, shape, dt, kind="ExternalInput").ap()
        DECL.append(name)

    def tmp(name, shape, dt=F32):
        kind = {"kind": "ExternalOutput"} if name in DBG else {}
        A[name] = nc.dram_tensor(name, shape, dt, **kind).ap()

    inp('x', [TL, D])
    for f in ('ffn1', 'ffn2'):
        inp(f + '_w_gate', [D, DFF]); inp(f + '_w_up', [D, DFF]); inp(f + '_w_down', [DFF, D])
    for n in ('ln1', 'ln2', 'ln3'):
        inp(n + '_g', [D]); inp(n + '_b', [D])
    inp('w_in_c', [D, 1024]); inp('w_out', [D, D])
    inp('conv_w_c', [128, 9]); inp('conv_b_c', [128, 3])
    inp('fw1', [33, 64]); inp('fw2', [64, 64]); inp('fw3', [64, 64]); inp('fbf', [64, 4])
    inp('fw_out_c', [64, 512]); inp('hbias_c', [128, 2]); inp('hyena_norm_g', [1024])
    for n in ('lq1', 'lk1', 'lq2', 'lk2'):
        inp(n, [64])
    inp('subln_g', [128])
    inp('zk', [33, NFFT]); inp('decay_k', [128, NFFT])
    inp('cos_tab', [128, L]); inp('sin_tab', [128, L])
    inp('F1', [128, 256], BF16); inp('F2', [128, 2, 256], BF16)
    inp('G', [128, 128, 256], BF16); inp('H', [128, 128, 2, 64], BF16)
    inp('identf', [128, 128]); inp('identb', [128, 128], BF16); inp('sel', [128, 8])
    A['y'] = nc.dram_tensor('y', [TL, D], F32, kind="ExternalOutput").ap()
    tmp('z_dram', [TL, D]); tmp('h_tok', [TL, D]); tmp('x2_tok', [TL, D])
    ag1_in = nc.dram_tensor('ag1_in', [D, TL], BF16)
    h_allT = nc.dram_tensor('h_allT', [NCORES * D, TL], BF16)
    A['ag1_in'] = ag1_in.ap(); A['h_allT'] = h_allT.ap()
    tmp('phy', [3, 128, L]); tmp('gd', [3, 128, L]); tmp('ud', [128, L], BF16)
    tmp('kd0', [128, NFFT], BF16); tmp('kd1', [128, NFFT], BF16)
    A['kd'] = [A['kd0'], A['kd1']]
    tmp('y1d', [128, L], BF16); tmp('y2d', [128, L], BF16); tmp('y1f', [128, L])
    a2a_in = nc.dram_tensor('a2a_in', [L, 256], BF16)
    a2a_out = nc.dram_tensor('a2a_out', [NCORES * L, 256], BF16)
    A['a2a_in'] = a2a_in.ap(); A['a2a_out'] = a2a_out.ap()
    tmp('a2a_dbg', [L, 256], BF16)
    with ExitStack() as es:
        tk = TK(nc, es)
        cc = es.enter_context(nc.semaphore("ccsem"))
        identf = es.enter_context(nc.sbuf_tensor("identf_sb", [128, 128], F32))
        identb = es.enter_context(nc.sbuf_tensor("identb_sb", [128, 128], BF16))
        es.enter_context(nc.Block())
        tk.dma('sp', identf[:], A['identf'], writes=['ident'])
        tk.dma('sp', identb[:], A['identb'], writes=['ident'])
        if TEST_MIXER:
            inp('hT_in', [D, TL], BF16)
            inp('h_tok_in', [TL, D])
            tk.dma('sp', A['ag1_in'], A['hT_in'], writes=['ag1'])
            tk.dma('sp', A['h_tok'], A['h_tok_in'], writes=['h_tok'])
            tk.barrier()
        else:
          ffn_ln(nc, tk, "1", A['x'], A['ffn1_w_gate'], A['ffn1_w_up'], A['ffn1_w_down'], A['ln1_g'], A['ln1_b'],
               A['z_dram'], A['h_tok'], A['ag1_in'], identf, identb)
        ccn = 0
        if STAGE >= 2:
            nc.gpsimd.collective_compute("AllGather", ALU.bypass, replica_groups=[list(range(NCORES))],
                                         ins=[ag1_in.ap().opt()], outs=[h_allT.ap().opt()]).then_inc(cc)
            ccn += 1
            for e in tk.engs.values():
                e.wait_ge(cc, ccn)
            mixer(nc, tk, A, identb)
        if STAGE >= 3:
            hyena(nc, tk, A, identb)
        if 'a2a_dbg' in DBG:
            tk.dma('sp', A['a2a_dbg'], A['a2a_in'], reads=['a2a_in'], writes=['a2a_dbg'])
            tk.barrier()
        if STAGE >= 4:
            nc.gpsimd.collective_compute("AllGather", ALU.bypass, replica_groups=[list(range(NCORES))],
                                         ins=[a2a_in.ap().opt()], outs=[a2a_out.ap().opt()]).then_inc(cc)
            ccn += 1
            for e in tk.engs.values():
                e.wait_ge(cc, ccn)
            phase_c(nc, tk, A, identb)
            if not TEST_MIXER:
              ffn_ln(nc, tk, "2", A['x2_tok'], A['ffn2_w_gate'], A['ffn2_w_up'], A['ffn2_w_down'], A['ln3_g'],
                   A['ln3_b'], A['z_dram'], A['y'], None, identf, identb)
        tk.barrier()
    return nc


STAGE = 4
DBG = ()
TEST_MIXER = False
DECL = []
TEST_H = None
RUN_KW = {}


def host_consts():
    c = {}
    bf = ml_dtypes.bfloat16
    a = np.arange(128, dtype=np.float64)
    ang = 2 * np.pi * np.outer(a, a) / 128.0
    c['F1'] = np.concatenate([np.cos(ang), -np.sin(ang)], axis=1).astype(bf)
    Er, Ei = np.cos(ang), np.sin(ang)
    c['F2'] = np.stack([np.concatenate([Er, Ei], 1), np.concatenate([-Ei, Er], 1)], axis=1).astype(bf)
    b = a[:, None, None]; k1 = a[None, :, None]; k2 = a[None, None, :]
    ang = 2 * np.pi * (b * (k1 + 128 * k2) % NFFT) / NFFT
    c['G'] = np.concatenate([np.cos(ang), -np.sin(ang)], axis=2).astype(bf)
    k1 = a[:, None, None]; b = a[None, :, None]; aa = np.arange(64, dtype=np.float64)[None, None, :]
    ang = 2 * np.pi * (((128 * aa + b) * k1) % NFFT) / NFFT
    c['H'] = np.stack([np.cos(ang), -np.sin(ang)], axis=2).astype(bf)
    c['identf'] = np.eye(128, dtype=np.float32)
    c['identb'] = np.eye(128, dtype=np.float32).astype(bf)
    inv = (500000.0 ** (-np.arange(0, 16, 2, dtype=np.float32) / 16)).astype(np.float32)
    pos = np.arange(L, dtype=np.float32)
    angr = (pos[:, None] * inv[None, :]).astype(np.float32)
    cs, sn = np.cos(angr).astype(np.float32), np.sin(angr).astype(np.float32)
    cos_tab = np.ones((128, L), np.float32); sin_tab = np.zeros((128, L), np.float32)
    for m in range(2):
        for d in range(16):
            cos_tab[m * 64 + d] = cs[:, d % 8]
            sin_tab[m * 64 + d] = (-sn[:, d] if d < 8 else sn[:, d - 8])
    c['cos_tab'] = cos_tab; c['sin_tab'] = sin_tab
    t = np.linspace(0.0, 1.0, L, dtype=np.float32)
    w = (2.0 * np.float32(math.pi) * np.arange(L, dtype=np.float32) / np.float32(L)).astype(np.float32)
    f = np.linspace(1e-4, 15.0, 16, dtype=np.float32)
    fw = (f[None, :] * w[:, None]).astype(np.float32)
    z = np.concatenate([t[:, None], np.cos(fw), -np.sin(fw)], axis=1).astype(np.float32)
    idx = np.concatenate([np.arange(L), [0], np.arange(L - 1, 0, -1)])
    c['zk'] = np.ascontiguousarray(z[idx].T)
    min_decay = math.log(1e-2) / 1.5
    max_decay = math.log(1e-2) / 0.3
    deltas = np.abs(np.linspace(min_decay, max_decay, 1024, dtype=np.float32))
    decay = np.exp(-t[:, None] * deltas[None, :]).astype(np.float32)
    dk = decay[idx]
    dk[L] = 0.0
    c['decay_all'] = np.ascontiguousarray(dk.T)
    return c


def kernel(**inputs):
    I = {k: np.asarray(v) for k, v in inputs.items()}
    hc = host_consts()
    del DECL[:]
    nc = build()
    w_in = I['w_in'][0]
    cw = I['hyena_conv_w'][0]; cbv = I['hyena_conv_b'][0]
    fwo = I['filt_w_out'][0]
    permd = np.arange(128)
    for m in range(2):
        for d in range(16):
            permd[m * 64 + d] = m * 64 + (d + 8 if d < 8 else d - 8)
    common = {
        'w_out': I['w_out'][0], 'hyena_norm_g': I['hyena_norm_g'][0],
        'fw1': I['filt_w1'][0], 'fw2': I['filt_w2'][0], 'fw3': I['filt_w3'][0],
        'fbf': np.ascontiguousarray(np.stack([I['filt_b1'][0], I['filt_b2'][0], I['filt_b3'][0], I['filt_freq'][0]], 1)),
        'lq1': I['lambda_q1'][0], 'lk1': I['lambda_k1'][0], 'lq2': I['lambda_q2'][0], 'lk2': I['lambda_k2'][0],
        'subln_g': I['subln_g'][0],
    }
    for f in ('ffn1', 'ffn2'):
        for s in ('_w_gate', '_w_up', '_w_down'):
            common[f + s] = I[f + s][0]
    for n in ('ln1', 'ln2', 'ln3'):
        common[n + '_g'] = I[n + '_g'][0]; common[n + '_b'] = I[n + '_b'][0]
    for k in ('zk', 'cos_tab', 'sin_tab', 'F1', 'F2', 'G', 'H', 'identf', 'identb'):
        common[k] = hc[k]
    in_maps = []
    for i in range(NCORES):
        sl = slice(i * 128, (i + 1) * 128)
        qc = w_in[:, 3072 + i * 128:3072 + (i + 1) * 128]
        kc = w_in[:, 4096 + i * 128:4096 + (i + 1) * 128]
        w_in_c = np.concatenate([w_in[:, sl], w_in[:, 1024 + i * 128:1024 + (i + 1) * 128],
                                 w_in[:, 2048 + i * 128:2048 + (i + 1) * 128], qc, kc,
                                 w_in[:, 5120 + i * 128:5120 + (i + 1) * 128], qc[:, permd], kc[:, permd]], axis=1)
        conv_w_c = np.stack([cw[tap, m * 1024 + i * 128:m * 1024 + (i + 1) * 128] for tap in range(3) for m in range(3)], 1)
        conv_b_c = np.stack([cbv[m * 1024 + i * 128:m * 1024 + (i + 1) * 128] for m in range(3)], 1)
        fw_out_c = np.concatenate([fwo[:, q * 1024 + i * 128:q * 1024 + (i + 1) * 128] for q in range(4)], 1)
        d = dict(common)
        d.update({
            'x': np.ascontiguousarray(I['x'][0, i * TL:(i + 1) * TL]),
            'w_in_c': np.ascontiguousarray(w_in_c), 'conv_w_c': np.ascontiguousarray(conv_w_c),
            'conv_b_c': np.ascontiguousarray(conv_b_c), 'fw_out_c': np.ascontiguousarray(fw_out_c),
            'hbias_c': np.ascontiguousarray(I['hyena_bias'][0][:, sl].T),
            'decay_k': np.ascontiguousarray(hc['decay_all'][sl]),
            'sel': np.ascontiguousarray(np.tile(np.eye(8, dtype=np.float32)[i][None, :], (128, 1))),
        })
        if TEST_MIXER:
            d['hT_in'] = TEST_H[i][0]; d['h_tok_in'] = TEST_H[i][1]
        in_maps.append({k: d[k] for k in DECL})
    res = run_bass_kernel_spmd(nc, in_maps, core_ids=list(range(NCORES)), **RUN_KW)
    kernel.last = res
    if TEST_MIXER:
        return None
    out = np.concatenate([np.asarray(r['y']) for r in res.results], axis=0)
    return out.reshape(1, L, D).astype(np.float32)
```

```python
import math
from contextlib import ExitStack
import numpy as np
import ml_dtypes
import concourse.bass as bass
import concourse.mybir as mybir
from concourse.bass_utils import run_bass_kernel_spmd

F32 = mybir.dt.float32
BF16 = mybir.dt.bfloat16
AF = mybir.ActivationFunctionType
ALU = mybir.AluOpType
AX = mybir.AxisListType

NCORES = 8
D = 2048
L = 8192
TL = L // NCORES
DFF = 5632
NJ = DFF // 128
ALPHA = 2.0 ** 0.25
LN_EPS = 1e-5
LAM_INIT = 0.8 - 0.6 * math.exp(0.0)
NFFT = 2 * L
CG = 32

DEBUG = False


class TK:
    def __init__(self, nc, es, n_dma_sems=40):
        self.nc = nc
        self.engs = {'pe': nc.tensor, 'dve': nc.vector, 'act': nc.scalar,
                     'pool': nc.gpsimd, 'sp': nc.sync}
        self.sem = {}
        for k in self.engs:
            self.sem[k] = es.enter_context(nc.semaphore("sem_" + k))
        self.cnt = {k: 0 for k in self.engs}
        self.known = {k: {} for k in self.engs}
        self.res = {}
        self.dsem = [es.enter_context(nc.semaphore("dsem%d" % i)) for i in range(n_dma_sems)]
        self.dcnt = [0] * n_dma_sems
        self.dnext = 0
        self.flip = 0

    def _get(self, key):
        r = self.res.get(key)
        if r is None:
            r = {'w': None, 'r': {}}
            self.res[key] = r
        return r

    def _sem(self, sk):
        return self.sem[sk] if isinstance(sk, str) else self.dsem[sk]

    def _wait(self, eng, evs):
        e = self.engs[eng]
        kn = self.known[eng]
        for (sk, val, _) in evs:
            if kn.get(sk, 0) >= val:
                continue
            e.wait_ge(self._sem(sk), val)
            kn[sk] = val

    def op(self, eng, fn, reads=(), writes=(), dma=False):
        need = []
        for key in reads:
            w = self._get(key)['w']
            if w is None:
                continue
            if (not dma) and w[2] == eng and eng == 'pe':
                continue
            need.append(w)
        for key in writes:
            r = self._get(key)
            w = r['w']
            if w is not None and (dma or w[2] != eng):
                need.append(w)
            for sk, (val, weng) in r['r'].items():
                if dma or weng != eng:
                    need.append((sk, val, weng))
        if dma:
            d = self.dnext
            self.dnext = (self.dnext + 1) % len(self.dsem)
            if self.dcnt[d] > 0:
                need.append((d, self.dcnt[d], 'dma'))
        self._wait(eng, need)
        ins = fn(self.engs[eng])
        if dma:
            self.dcnt[d] += 16
            ins.then_inc(self.dsem[d], 16)
            ev = (d, self.dcnt[d], 'dma')
        else:
            self.cnt[eng] += 1
            ins.then_inc(self.sem[eng], 1)
            ev = (eng, self.cnt[eng], eng)
        for key in writes:
            self.res[key] = {'w': ev, 'r': {}}
        for key in reads:
            rr = self._get(key)['r']
            old = rr.get(ev[0])
            if old is None or old[0] < ev[1]:
                rr[ev[0]] = (ev[1], ev[2])
        return ins

    def dma(self, q, out, in_, reads=(), writes=()):
        return self.op(q, lambda e: e.dma_start(out=out, in_=in_), reads, writes, dma=True)

    def alt(self):
        self.flip ^= 1
        return 'act' if self.flip else 'dve'

    def copy(self, eng, out, in_, reads=(), writes=()):
        if eng == 'act':
            return self.op('act', lambda e: e.copy(out=out, in_=in_), reads, writes)
        return self.op(eng, lambda e: e.tensor_copy(out=out, in_=in_), reads, writes)

    def barrier(self):
        evs = [(k, self.cnt[k], k) for k in self.engs if self.cnt[k] > 0]
        evs += [(d, c, 'dma') for d, c in enumerate(self.dcnt) if c > 0]
        for k in self.engs:
            self._wait(k, [e for e in evs if e[0] != k])
        self.res = {}


def bcast_row(ap1d, parts, n):
    return ap1d.rearrange("(o n) -> o n", o=1).broadcast(0, parts) if hasattr(ap1d, "broadcast") else None


def ffn_ln(nc, tk, tag, x_dram, Wg, Wu, Wd, g_dram, b_dram, z_dram, out_tok, outT, identf, identb):
    with ExitStack() as es:
        sb = lambda name, shape, dt: es.enter_context(nc.sbuf_tensor(name + tag, shape, dt))
        xT = sb("xT", [128, 16, TL], BF16)
        h1T = sb("h1T", [128, NJ, TL], BF16)
        ps = [es.enter_context(nc.psum_tensor("ps%s%d" % (tag, i), [128, 512], F32)) for i in range(8)]
        xin = [sb("xin%d" % i, [128, 2048], F32) for i in range(2)]
        wg = [sb("wg%d" % i, [128, 16, 256], BF16) for i in range(2)]
        wu = [sb("wu%d" % i, [128, 16, 256], BF16) for i in range(2)]
        wd = [sb("wd%d" % i, [128, 4, 512], BF16) for i in range(3)]
        sg = [sb("sg%d" % i, [128, 512], F32) for i in range(2)]
        for t in range(8):
            buf = xin[t % 2]
            tk.dma('sp', buf[:], x_dram[t * 128:(t + 1) * 128, :], writes=[('xin', t % 2)])
            for kg in range(4):
                bi = (t * 4 + kg) % 8
                bank = ps[bi]
                for kk in range(4):
                    k = kg * 4 + kk
                    tk.op('pe', lambda e, kk=kk, k=k, bank=bank, buf=buf: e.transpose(
                        out=bank[:, kk * 128:(kk + 1) * 128], in_=buf[:, k * 128:(k + 1) * 128],
                        identity=identf[:]), reads=[('xin', t % 2), 'ident'], writes=[('ps', bi)])
                tk.copy(tk.alt(), xT[:, kg * 4:(kg + 1) * 4, t * 128:(t + 1) * 128],
                        bank[:].rearrange("p (a b) -> p a b", a=4),
                        reads=[('ps', bi)], writes=[('xT', t)])
        pi = 0
        for c in range(NJ // 2):
            s = c % 2
            tk.dma('pool', wg[s][:], Wg[:, c * 256:(c + 1) * 256].rearrange("(k p) n -> p k n", p=128),
                   writes=[('wg', s)])
            tk.dma('pool', wu[s][:], Wu[:, c * 256:(c + 1) * 256].rearrange("(k p) n -> p k n", p=128),
                   writes=[('wu', s)])
            for sub in range(2):
                j = c * 2 + sub
                for hh in range(2):
                    bg, bu = (pi * 2) % 8, (pi * 2 + 1) % 8
                    pi += 1
                    xr = [('xT', t) for t in range(hh * 4, hh * 4 + 4)]
                    for k in range(16):
                        tk.op('pe', lambda e, k=k, bg=bg, s=s, sub=sub, hh=hh: e.matmul(
                            ps[bg][:], wg[s][:, k, sub * 128:(sub + 1) * 128],
                            xT[:, k, hh * 512:(hh + 1) * 512], start=(k == 0), stop=(k == 15)),
                            reads=[('wg', s)] + xr, writes=[('ps', bg)])
                    for k in range(16):
                        tk.op('pe', lambda e, k=k, bu=bu, s=s, sub=sub, hh=hh: e.matmul(
                            ps[bu][:], wu[s][:, k, sub * 128:(sub + 1) * 128],
                            xT[:, k, hh * 512:(hh + 1) * 512], start=(k == 0), stop=(k == 15)),
                            reads=[('wu', s)] + xr, writes=[('ps', bu)])
                    si = pi % 2
                    tk.op('act', lambda e, bg=bg, si=si: e.activation(out=sg[si][:], in_=ps[bg][:], func=AF.Silu),
                          reads=[('ps', bg)], writes=[('sg', si)])
                    tk.op('dve', lambda e, bu=bu, si=si, j=j, hh=hh: e.tensor_tensor(
                        out=h1T[:, j, hh * 512:(hh + 1) * 512], in0=sg[si][:], in1=ps[bu][:], op=ALU.mult),
                        reads=[('sg', si), ('ps', bu)], writes=[('h1T', j, hh)])
        xres = [sb("xres%d" % i, [128, 512], F32) for i in range(2)]
        zt = [sb("zt%d" % i, [128, 512], F32) for i in range(2)]
        wi = 0
        for c in range(4):
            for jg in range(NJ // 4):
                s = wi % 3
                wi += 1
                tk.dma('pool', wd[s][:], Wd[jg * 512:(jg + 1) * 512, c * 512:(c + 1) * 512].rearrange(
                    "(j p) n -> p j n", p=128), writes=[('wd', s)])
                for jj in range(4):
                    j = jg * 4 + jj
                    for t in range(8):
                        tk.op('pe', lambda e, j=j, jj=jj, t=t, s=s: e.matmul(
                            ps[t][:], h1T[:, j, t * 128:(t + 1) * 128], wd[s][:, jj, :],
                            start=(j == 0), stop=(j == NJ - 1)),
                            reads=[('wd', s), ('h1T', j, t // 4)], writes=[('ps', t)])
            for t in range(8):
                r = t % 2
                tk.dma('sp', xres[r][:], x_dram[t * 128:(t + 1) * 128, c * 512:(c + 1) * 512],
                       writes=[('xres', r)])
                tk.op('act', lambda e, r=r: e.mul(out=xres[r][:], in_=xres[r][:], mul=ALPHA) if False else
                      e.activation(out=xres[r][:], in_=xres[r][:], func=AF.Copy, scale=ALPHA),
                      reads=[('xres', r)], writes=[('xres', r)])
                tk.op('dve', lambda e, r=r, t=t: e.scalar_tensor_tensor(
                    out=zt[r][:], in0=ps[t][:], scalar=0.5, in1=xres[r][:], op0=ALU.mult, op1=ALU.add),
                    reads=[('ps', t), ('xres', r)], writes=[('zt', r)])
                tk.dma('sp', z_dram[t * 128:(t + 1) * 128, c * 512:(c + 1) * 512], zt[r][:],
                       reads=[('zt', r)], writes=[('zd', t)])
    tk.barrier()
    with ExitStack() as es:
        ps = [es.enter_context(nc.psum_tensor("psl%s%d" % (tag, i), [128, 512], F32)) for i in range(4)]
        layer_norm_rows(nc, tk, es, tag, z_dram, g_dram, b_dram, out_tok, outT, ps, identb, zkeys=False)
    tk.barrier()


def layer_norm_rows(nc, tk, es, tag, z_dram, g_dram, b_dram, out_tok, outT, ps, identb, zkeys=True):
    sb = lambda name, shape, dt: es.enter_context(nc.sbuf_tensor(name + tag + "ln", shape, dt))
    gt = sb("gt", [128, 2048], F32)
    bt = sb("bt", [128, 2048], F32)
    tk.dma('sp', gt[:], g_dram.partition_broadcast(128), writes=['gt'])
    tk.dma('sp', bt[:], b_dram.partition_broadcast(128), writes=['bt'])
    zin = [sb("zin%d" % i, [128, 2048], F32) for i in range(2)]
    st = sb("st", [128, 4, 6], F32)
    mv = sb("mv", [128, 2], F32)
    rstd = sb("rstd", [128, 1], F32)
    ob = [sb("ob%d" % i, [128, 2048], BF16) for i in range(2)]
    oT = sb("oT", [128, 16, 128], BF16) if outT is not None else None
    for t in range(8):
        r = t % 2
        z = zin[r]
        tk.dma('sp', z[:], z_dram[t * 128:(t + 1) * 128, :], reads=[('zd', t)], writes=[('zin', r)])
        for q in range(4):
            tk.op('dve', lambda e, q=q, z=z: e.bn_stats(out=st[:, q, :], in_=z[:, q * 512:(q + 1) * 512]),
                  reads=[('zin', r)], writes=[('st', q)])
        tk.op('dve', lambda e: e.bn_aggr(out=mv[:], in_=st[:].rearrange("p a b -> p (a b)")),
              reads=[('st', q) for q in range(4)], writes=['mv'])
        tk.op('act', lambda e: e.activation(out=rstd[:], in_=mv[:, 1:2], func=AF.Sqrt, bias=LN_EPS, scale=1.0),
              reads=['mv'], writes=['rstd'])
        tk.op('dve', lambda e: e.reciprocal(out=rstd[:], in_=rstd[:]), reads=['rstd'], writes=['rstd'])
        tk.op('dve', lambda e, z=z: e.tensor_scalar(out=z[:], in0=z[:], scalar1=mv[:, 0:1], scalar2=rstd[:, 0:1],
                                                    op0=ALU.subtract, op1=ALU.mult),
              reads=[('zin', r), 'mv', 'rstd'], writes=[('zin', r)])
        tk.op('pool', lambda e, z=z: e.tensor_tensor(out=z[:], in0=z[:], in1=gt[:], op=ALU.mult),
              reads=[('zin', r), 'gt'], writes=[('zin', r)])
        tk.op('pool', lambda e, z=z: e.tensor_tensor(out=z[:], in0=z[:], in1=bt[:], op=ALU.add),
              reads=[('zin', r), 'bt'], writes=[('zin', r)])
        tk.dma('sp', out_tok[t * 128:(t + 1) * 128, :], z[:], reads=[('zin', r)], writes=[('otok', t)])
        if outT is not None:
            tk.copy('act', ob[r][:], z[:], reads=[('zin', r)], writes=[('ob', r)])
            for half in range(2):
                bi = (t % 2) * 2 + half
                pbv = ps[bi][:].bitcast(BF16)
                for kk in range(8):
                    k = half * 8 + kk
                    tk.op('pe', lambda e: e.transpose(out=pbv[:, kk * 128:(kk + 1) * 128],
                                                      in_=ob[r][:, k * 128:(k + 1) * 128], identity=identb[:]),
                          reads=[('ob', r), 'ident'], writes=[('ps', bi)])
                tk.copy('dve', oT[:, half * 8:(half + 1) * 8, :], pbv.rearrange("p (k n) -> p k n", k=8),
                        reads=[('ps', bi)], writes=[('oT', half)])
            tk.dma('sp', outT[:, t * 128:(t + 1) * 128].rearrange("(k p) n -> p k n", p=128), oT[:],
                   reads=[('oT', 0), ('oT', 1)], writes=[('outT', t)])


def mixer(nc, tk, A, identb):
    with ExitStack() as es:
        sb = lambda name, shape, dt: es.enter_context(nc.sbuf_tensor("m_" + name, shape, dt))
        ps = [es.enter_context(nc.psum_tensor("mps%d" % i, [128, 512], F32)) for i in range(8)]
        Wsl = sb("Wsl", [128, 16, 1024], BF16)
        for i in range(4):
            tk.dma('pool', Wsl[:, :, i * 256:(i + 1) * 256],
                   A['w_in_c'][:, i * 256:(i + 1) * 256].rearrange("(k p) n -> p k n", p=128), writes=[('Wsl', i)])
        WR = [('Wsl', i) for i in range(4)]
        qT = sb("qT", [128, L], BF16)
        kT = sb("kT", [128, L], BF16)
        Vaug = sb("Vaug", [128, 64, 130], BF16)
        tk.op('pool', lambda e: e.memset(Vaug[:, :, 128:130], 1.0), writes=['Vones'])
        hTb = [sb("hTb%d" % i, [128, 16, 512], BF16) for i in range(2)]
        cosb = [sb("cosb%d" % i, [128, 512], F32) for i in range(2)]
        sinb = [sb("sinb%d" % i, [128, 512], F32) for i in range(2)]
        stg = [sb("stg%d" % i, [128, 3, 512], F32) for i in range(2)]
        t1 = [sb("t1_%d" % i, [128, 512], F32) for i in range(2)]
        t2 = [sb("t2_%d" % i, [128, 512], F32) for i in range(2)]
        lq = sb("lq", [128, 4, 64], F32)
        for i, nm in enumerate(['lq1', 'lk1', 'lq2', 'lk2']):
            tk.dma('sp', lq[:, i, :], A[nm].partition_broadcast(128), writes=[('lq', i)])
        lp = sb("lp", [128, 2], F32)
        ljunk = sb("ljunk", [128, 64], F32)
        for i in range(2):
            tk.op('dve', lambda e, i=i: e.scalar_tensor_tensor(
                out=ljunk[:], in0=lq[:, 2 * i, :], scalar=1.0, in1=lq[:, 2 * i + 1, :],
                op0=ALU.mult, op1=ALU.mult, accum_out=lp[:, i:i + 1]),
                reads=[('lq', 2 * i), ('lq', 2 * i + 1)], writes=[('lp', i), 'ljunk'])
        le = sb("le", [128, 2], F32)
        tk.op('act', lambda e: e.activation(out=le[:], in_=lp[:], func=AF.Exp),
              reads=[('lp', 0), ('lp', 1)], writes=['le'])
        neglam = sb("neglam", [128, 1], F32)
        tk.op('dve', lambda e: e.scalar_tensor_tensor(out=neglam[:], in0=le[:, 1:2], scalar=-LAM_INIT,
                                                      in1=le[:, 0:1], op0=ALU.add, op1=ALU.subtract),
              reads=['le'], writes=['neglam'])
        gsub = sb("gsub", [128, 128], F32)
        tk.dma('sp', gsub[:], A['subln_g'].partition_broadcast(128), writes=['gsub'])
        tk.op('dve', lambda e: e.tensor_scalar(out=gsub[:], in0=gsub[:], scalar1=(1.0 - LAM_INIT), scalar2=None,
                                               op0=ALU.mult), reads=['gsub'], writes=['gsub'])
        PI = math.pi
        fw1 = sb("fw1", [33, 64], F32)
        fw2 = sb("fw2", [64, 64], F32)
        fw3 = sb("fw3", [64, 64], F32)
        ffb = sb("ffb", [64, 4], F32)
        tk.dma('sp', fw1[:], A['fw1'], writes=['fw1'])
        tk.dma('sp', fw2[:], A['fw2'], writes=['fw2'])
        tk.dma('sp', fw3[:], A['fw3'], writes=['fw3'])
        tk.dma('sp', ffb[:], A['fbf'], writes=['ffb'])
        fwo = sb("fwo", [64, 512], BF16)
        tk.dma('pool', fwo[:], A['fw_out_c'], writes=['fwo'])
        zkb = [sb("zkb%d" % i, [33, 512], F32) for i in range(2)]
        fta = [sb("fta%d" % i, [64, 512], F32) for i in range(3)]
        ftm = [sb("ftm%d" % i, [64, 512], F32) for i in range(3)]
        fhh = [[sb("fhh%d_%d" % (li, i), [64, 512], F32) for i in range(2)] for li in range(2)]
        fh3 = [sb("fh3_%d" % i, [64, 512], BF16) for i in range(2)]
        fdec = [sb("fdec%d" % i, [128, 512], F32) for i in range(2)]
        fkb = [sb("fkb%d" % i, [128, 512], BF16) for i in range(2)]
        fcnt = [0, 0]

        def filt_layer(li, c):
            p = c % 2
            b = 6 + (fcnt[0] % 2)
            fcnt[0] += 1
            if li == 0:
                tk.dma('sp', zkb[p][:], A['zk'][:, c * 512:(c + 1) * 512], writes=[('zkb', p)])
                w, wk, src, sk, kk = fw1, 'fw1', zkb[p], ('zkb', p), 33
            else:
                w, wk = (fw2, 'fw2') if li == 1 else (fw3, 'fw3')
                src, sk, kk = fhh[li - 1][p], ('fhh', li - 1, p), 64
            ta, tm = fta[li], ftm[li]
            tk.op('pe', lambda e: e.matmul(ps[b][0:64, :], w[0:kk, :], src[0:kk, :], start=True, stop=True),
                  reads=[wk, sk], writes=[('ps', b)])
            tk.op('dve', lambda e: e.tensor_scalar(out=ta[:], in0=ps[b][0:64, :], scalar1=ffb[:, li:li + 1],
                                                   scalar2=ffb[:, 3:4], op0=ALU.add, op1=ALU.mult),
                  reads=[('ps', b), 'ffb'], writes=[('fta', li)])
            tk.op('dve', lambda e: e.tensor_scalar(out=tm[:], in0=ta[:], scalar1=PI, scalar2=-2 * PI,
                                                   op0=ALU.is_gt, op1=ALU.mult), reads=[('fta', li)], writes=[('ftm', li)])
            tk.op('dve', lambda e: e.tensor_tensor(out=ta[:], in0=ta[:], in1=tm[:], op=ALU.add),
                  reads=[('fta', li), ('ftm', li)], writes=[('fta', li)])
            tk.op('dve', lambda e: e.tensor_scalar(out=tm[:], in0=ta[:], scalar1=-PI, scalar2=2 * PI,
                                                   op0=ALU.is_lt, op1=ALU.mult), reads=[('fta', li)], writes=[('ftm', li)])
            tk.op('dve', lambda e: e.tensor_tensor(out=ta[:], in0=ta[:], in1=tm[:], op=ALU.add),
                  reads=[('fta', li), ('ftm', li)], writes=[('fta', li)])
            if li < 2:
                tk.op('act', lambda e: e.activation(out=fhh[li][p][:], in_=ta[:], func=AF.Sin),
                      reads=[('fta', li)], writes=[('fhh', li, p)])
            else:
                tk.op('act', lambda e: e.activation(out=fh3[p][:], in_=ta[:], func=AF.Sin),
                      reads=[('fta', li)], writes=[('fh3', p)])

        def filt_kbuild(c):
            p = c % 2
            dr = c // 16
            tk.dma('sp', fdec[p][:], A['decay_k'][:, c * 512:(c + 1) * 512], writes=[('fdec', p)])
            for n in range(2):
                b = 6 + (fcnt[0] % 2)
                fcnt[0] += 1
                tk.op('pe', lambda e: e.matmul(ps[b][:], fwo[:, (n * 2 + dr) * 128:(n * 2 + dr + 1) * 128],
                                               fh3[p][:], start=True, stop=True),
                      reads=['fwo', ('fh3', p)], writes=[('ps', b)])
                q = fcnt[1] % 2
                fcnt[1] += 1
                tk.op('dve', lambda e: e.tensor_tensor(out=fkb[q][:], in0=ps[b][:], in1=fdec[p][:], op=ALU.mult),
                      reads=[('ps', b), ('fdec', p)], writes=[('fkb', q)])
                tk.dma('sp', A['kd'][n][:, c * 512:(c + 1) * 512], fkb[q][:], reads=[('fkb', q)], writes=[('kd', n)])

        def filt_step(h):
            NCH = NFFT // 512
            if 0 <= h < NCH:
                filt_layer(0, h)
            if 0 <= h - 1 < NCH:
                filt_layer(1, h - 1)
            if 0 <= h - 2 < NCH:
                filt_layer(2, h - 2)
            if 0 <= h - 3 < NCH:
                filt_kbuild(h - 3)

        pi = 0
        for blk in range(16):
            s = blk % 2
            filt_step(2 * blk)
            tk.dma('sp', hTb[s][:], A['h_allT'][(blk // 2) * D:(blk // 2 + 1) * D,
                                                (blk % 2) * 512:(blk % 2 + 1) * 512].rearrange(
                "(k p) n -> p k n", p=128), reads=['h_allT'], writes=[('hTb', s)])
            tk.dma('sp', cosb[s][:], A['cos_tab'][:, blk * 512:(blk + 1) * 512], writes=[('cosb', s)])
            tk.dma('sp', sinb[s][:], A['sin_tab'][:, blk * 512:(blk + 1) * 512], writes=[('sinb', s)])

            def proj(col0, bank):
                for k in range(16):
                    tk.op('pe', lambda e, k=k: e.matmul(ps[bank][:], Wsl[:, k, col0:col0 + 128], hTb[s][:, k, :],
                                                       start=(k == 0), stop=(k == 15)),
                          reads=WR + [('hTb', s)], writes=[('ps', bank)])
            for m in range(3):
                b = pi % 6
                pi += 1
                proj(m * 128, b)
                tk.copy(tk.alt(), stg[s][:, m, :], ps[b][:], reads=[('ps', b)], writes=[('stg', s, m)])
            tk.dma('sp', A['phy'].rearrange("m c t -> c m t")[:, :, blk * 512:(blk + 1) * 512], stg[s][:],
                   reads=[('stg', s, m) for m in range(3)], writes=['phy'])
            filt_step(2 * blk + 1)
            for (dst, c0, cp, nm) in ((qT, 384, 768, 'qT'), (kT, 512, 896, 'kT')):
                ba, bb = pi % 6, (pi + 1) % 6
                pi += 2
                proj(c0, ba)
                proj(cp, bb)
                tk.op('dve', lambda e: e.tensor_tensor(out=t1[s][:], in0=ps[ba][:], in1=cosb[s][:], op=ALU.mult),
                      reads=[('ps', ba), ('cosb', s)], writes=[('t1', s)])
                tk.op('dve', lambda e: e.tensor_tensor(out=t2[s][:], in0=ps[bb][:], in1=sinb[s][:], op=ALU.mult),
                      reads=[('ps', bb), ('sinb', s)], writes=[('t2', s)])
                tk.op('pool', lambda e, dst=dst: e.tensor_tensor(out=dst[:, blk * 512:(blk + 1) * 512], in0=t1[s][:],
                                                                 in1=t2[s][:], op=ALU.add),
                      reads=[('t1', s), ('t2', s)], writes=[(nm, blk)])
            b = pi % 6
            pi += 1
            for tt in range(4):
                for k in range(16):
                    tk.op('pe', lambda e, k=k, tt=tt: e.matmul(
                        ps[b][:, tt * 128:(tt + 1) * 128], hTb[s][:, k, tt * 128:(tt + 1) * 128], Wsl[:, k, 640:768],
                        start=(k == 0 and tt == 0), stop=(k == 15), skip_group_check=True),
                        reads=WR + [('hTb', s)], writes=[('ps', b)])
            tk.copy('act', Vaug[:, blk * 4:(blk + 1) * 4, 0:128], ps[b][:].rearrange("p (a b) -> p a b", a=4),
                    reads=[('ps', b)], writes=[('V', blk)])
        for h in range(32, 35):
            filt_step(h)
        E = [sb("E%d" % i, [128, 2, 512], BF16) for i in range(2)]
        rc = sb("rc", [128, 2], F32)
        c2 = sb("c2", [128, 1], F32)
        o1 = sb("o1", [128, 128], F32)
        o2 = sb("o2", [128, 128], F32)
        ojunk = sb("ojunk", [128, 128], F32)
        ssq = sb("ssq", [128, 1], F32)
        rstd = sb("rstda", [128, 1], F32)
        ob = [sb("oba%d" % i, [128, 128], BF16) for i in range(2)]
        VR = [('V', b) for b in range(16)] + ['Vones']
        oi = 0
        osb = [sb("osb%d" % i, [128, 4, 320], F32) for i in range(2)]

        def s_mm(Q, kt):
            sbi = kt % 2
            for m in range(2):
                tk.op('pe', lambda e, m=m: e.matmul(
                    ps[sbi * 2 + m][:], kT[m * 64:(m + 1) * 64, kt * 128:(kt + 1) * 128],
                    qT[m * 64:(m + 1) * 64, Q * 512:(Q + 1) * 512], start=True, stop=True),
                    reads=[('kT', kt // 4), ('qT', Q)], writes=[('ps', sbi * 2 + m)])

        o2s = [sb("o2s%d" % i, [128, 4, 128], F32) for i in range(2)]
        ssq4 = [sb("ssq4_%d" % i, [128, 4], F32) for i in range(2)]
        rstd4 = [sb("rstd4_%d" % i, [128, 4], F32) for i in range(2)]

        def epi1(Q):
            p = Q % 2
            ob_ = osb[p]
            for qt in range(4):
                tk.copy('dve', ob_[:, qt, 0:289], ps[4 + qt][:, 0:289], reads=[('ps', 4 + qt)],
                        writes=[('osb', p, qt)])
            for qt in range(4):
                pb = ob_[:, qt, :]
                PR = [('osb', p, qt)]
                for m in range(2):
                    tk.op('dve', lambda e, m=m: e.reciprocal(out=rc[:, m:m + 1], in_=pb[:, m * 160 + 128:m * 160 + 129]),
                          reads=PR, writes=[('rc', m)])
                tk.op('dve', lambda e: e.tensor_tensor(out=c2[:], in0=rc[:, 1:2], in1=neglam[:], op=ALU.mult),
                      reads=[('rc', 1), 'neglam'], writes=['c2'])
                tk.op('dve', lambda e: e.tensor_scalar(out=o1[:], in0=pb[:, 0:128], scalar1=rc[:, 0:1], scalar2=None,
                                                       op0=ALU.mult), reads=PR + [('rc', 0)], writes=['o1'])
                tk.op('dve', lambda e: e.scalar_tensor_tensor(out=o2s[p][:, qt, :], in0=pb[:, 160:288], scalar=c2[:, 0:1],
                                                              in1=o1[:], op0=ALU.mult, op1=ALU.add),
                      reads=PR + ['c2', 'o1'], writes=[('o2s', p, qt)])
                tk.op('dve', lambda e: e.scalar_tensor_tensor(out=ojunk[:], in0=o2s[p][:, qt, :], scalar=1.0,
                                                              in1=o2s[p][:, qt, :], op0=ALU.mult, op1=ALU.mult,
                                                              accum_out=ssq4[p][:, qt:qt + 1]),
                      reads=[('o2s', p, qt)], writes=['ojunk', ('ssq4', p, qt)])

        def epi2(Q):
            nonlocal oi
            p = Q % 2
            tk.op('act', lambda e: e.activation(out=rstd4[p][:], in_=ssq4[p][:], func=AF.Sqrt, bias=LN_EPS,
                                                scale=1.0 / 128),
                  reads=[('ssq4', p, qt) for qt in range(4)], writes=[('rstd4', p)])
            tk.op('dve', lambda e: e.reciprocal(out=rstd4[p][:], in_=rstd4[p][:]), reads=[('rstd4', p)],
                  writes=[('rstd4', p)])
            for qt in range(4):
                o = oi % 2
                oi += 1
                tk.op('dve', lambda e: e.scalar_tensor_tensor(out=ob[o][:], in0=o2s[p][:, qt, :],
                                                              scalar=rstd4[p][:, qt:qt + 1], in1=gsub[:],
                                                              op0=ALU.mult, op1=ALU.mult),
                      reads=[('o2s', p, qt), ('rstd4', p), 'gsub'], writes=[('oba', o)])
                r0 = Q * 512 + qt * 128
                tk.dma('sp', A['a2a_in'][r0:r0 + 128, 128:256], ob[o][:], reads=[('oba', o)], writes=['a2a_in'])

        for Q in range(16):
            s_mm(Q, 0)
            for kt in range(64):
                sbi = kt % 2
                for m in range(2):
                    tk.op('act', lambda e, m=m: e.activation(out=E[sbi][:, m, :], in_=ps[sbi * 2 + m][:],
                                                             func=AF.Exp, scale=0.125),
                          reads=[('ps', sbi * 2 + m)], writes=[('E', sbi, m)])
                if kt + 1 < 64:
                    s_mm(Q, kt + 1)
                for qt in range(4):
                    for m in range(2):
                        tk.op('pe', lambda e, m=m, qt=qt: e.matmul(
                            ps[4 + qt][:, m * 160:m * 160 + 129], E[sbi][:, m, qt * 128:(qt + 1) * 128],
                            Vaug[:, kt, 0:129], start=(kt == 0 and m == 0), stop=(kt == 63), skip_group_check=True),
                            reads=[('E', sbi, m)] + (VR if kt == 0 else []), writes=[('ps', 4 + qt)])
                if kt == 12 and Q > 0:
                    epi2(Q - 1)
            epi1(Q)
        epi2(15)
    tk.barrier()


def hyena(nc, tk, A, identb):
    PI = math.pi
    with ExitStack() as es:
        sb = lambda name, shape, dt: es.enter_context(nc.sbuf_tensor("h1_" + name, shape, dt))
        cw = sb("cw", [128, 9], F32)
        cb = sb("cb", [128, 3], F32)
        tk.dma('sp', cw[:], A['conv_w_c'], writes=['cw'])
        tk.dma('sp', cb[:], A['conv_b_c'], writes=['cb'])
        pin = sb("pin", [128, L + 2], F32)
        zc = sb("zc", [128, L], F32)
        ub = sb("ub", [128, L], BF16)
        tk.op('pool', lambda e: e.memset(pin[:, 0:1], 0.0), writes=['pin0'])
        tk.op('pool', lambda e: e.memset(pin[:, L + 1:L + 2], 0.0), writes=['pin1'])
        for m in range(3):
            tk.dma('sp', pin[:, 1:L + 1], A['phy'][m], reads=['phy'], writes=['pin'])
            R = ['pin', 'pin0', 'pin1', 'cw', 'cb']
            tk.op('dve', lambda e: e.tensor_scalar(out=zc[:], in0=pin[:, 1:L + 1], scalar1=cw[:, 3 + m:4 + m],
                                                   scalar2=cb[:, m:m + 1], op0=ALU.mult, op1=ALU.add),
                  reads=R, writes=['zc'])
            tk.op('dve', lambda e: e.scalar_tensor_tensor(out=zc[:], in0=pin[:, 0:L], scalar=cw[:, m:m + 1],
                                                          in1=zc[:], op0=ALU.mult, op1=ALU.add),
                  reads=R + ['zc'], writes=['zc'])
            tk.op('dve', lambda e: e.scalar_tensor_tensor(out=zc[:], in0=pin[:, 2:L + 2], scalar=cw[:, 6 + m:7 + m],
                                                          in1=zc[:], op0=ALU.mult, op1=ALU.add),
                  reads=R + ['zc'], writes=['zc'])
            if m == 0:
                tk.copy('act', ub[:], zc[:], reads=['zc'], writes=['ub'])
                tk.dma('sp', A['ud'], ub[:], reads=['ub'], writes=['ud'])
            tk.dma('sp', A['gd'][m], zc[:], reads=['zc'], writes=[('gd', m)])
    tk.barrier()
    with ExitStack() as es:
        sb = lambda name, shape, dt: es.enter_context(nc.sbuf_tensor("h3_" + name, shape, dt))
        psd = [es.enter_context(nc.psum_tensor("h3psd%d" % i, [128, 1024], F32)) for i in range(4)]
        ps = [psd[i // 2][:, (i % 2) * 512:(i % 2 + 1) * 512] for i in range(8)]
        F1 = sb("F1", [128, 256], BF16)
        F2 = sb("F2", [128, 2, 256], BF16)
        H = sb("H", [128, 128, 2, 64], BF16)
        tk.dma('sp', F1[:], A['F1'], writes=['F1'])
        tk.dma('sp', F2[:], A['F2'], writes=['F2'])
        for i in range(4):
            tk.dma('sp', H[:, i * 32:(i + 1) * 32], A['H'][:, i * 32:(i + 1) * 32], writes=[('H', i)])
        HR = [('H', i) for i in range(4)]
        Y = sb("Y", [128, CG, 2, 128], BF16)
        Qs = sb("Qs", [128, 128, 2, CG], BF16)
        hb = sb("hb", [CG, 2], F32)
        pi = 0
        gi = 0
        for g in range(128 // CG):
            c0 = g * CG
            tk.dma('sp', hb[:], A['hbias_c'][c0:c0 + CG, :], writes=['hb'])
            for n in range(2):
              with ExitStack() as es2:
                sb2 = lambda name, shape, dt: es2.enter_context(nc.sbuf_tensor("h3a%d%d_" % (g, n) + name, shape, dt))
                Gb = [sb2("Gb%d" % i, [128, 8, 256], BF16) for i in range(3)]
                Du = sb2("Du", [64, CG, 128], BF16)
                Dk = sb2("Dk", [128, CG, 128], BF16)
                AA = sb2("AA", [128, 2 * CG, 384], BF16)
                Au = AA[:, 0:CG, :]
                Ak = AA[:, CG:2 * CG, :]
                Kc = [sb2("Kc%d" % i, [128, 512], F32) for i in range(2)]
                P1 = [sb2("P1_%d" % i, [128, 512], F32) for i in range(2)]
                T3 = [sb2("T3_%d" % i, [128, 256], F32) for i in range(2)]
                T4 = [sb2("T4_%d" % i, [128, 256], F32) for i in range(2)]
                usrc = A['ud'] if n == 0 else A['y1d']
                tk.dma('sp', Du[:], usrc[c0:c0 + CG, :].rearrange("c (a b) -> a c b", b=128),
                       reads=['ud' if n == 0 else 'y1d'], writes=['Du'])
                tk.dma('sp', Dk[:], A['kd'][n][c0:c0 + CG, :].rearrange("c (a b) -> a c b", b=128),
                       reads=[('kd', n)], writes=['Dk'])
                for (Dt, At, KA, dk, ak) in ((Du, Au, 64, 'Du', 'Au'), (Dk, Ak, 128, 'Dk', 'Ak')):
                    for cp in range(CG // 2):
                        b = pi % 8
                        pi += 1
                        for i in range(2):
                            tk.op('pe', lambda e: e.matmul(ps[b][:, i * 256:(i + 1) * 256], Dt[0:KA, cp * 2 + i, :],
                                                           F1[0:KA, :], start=(i == 0), stop=True, skip_group_check=True),
                                  reads=[dk, 'F1'], writes=[('psd', b // 2)])
                        pv = ps[b][:].rearrange("p (i x) -> p i x", i=2)
                        tk.copy('dve', At[:, cp * 2:cp * 2 + 2, 0:256], pv, reads=[('psd', b // 2)], writes=[ak])
                        tk.op('act', lambda e: e.activation(out=At[:, cp * 2:cp * 2 + 2, 256:384], in_=pv[:, :, 128:256],
                                                            func=AF.Copy, scale=-1.0), reads=[('psd', b // 2)], writes=[ak])
                for kc in range(16):
                    gs = gi % 3
                    gi += 1
                    tk.dma('sp', Gb[gs][:], A['G'][:, kc * 8:(kc + 1) * 8, :], writes=[('Gb', gs)])
                    d = pi % 4
                    pi += 1
                    pd = psd[d]
                    for i in range(8):
                        k1 = kc * 8 + i
                        Ar, Ai, nAi = AA[:, :, k1], AA[:, :, 128 + k1], AA[:, :, 256 + k1]
                        Gr, Gi = Gb[gs][:, i, 0:128], Gb[gs][:, i, 128:256]
                        o_r = pd[:, i * 128:i * 128 + 64]
                        o_i = pd[:, i * 128 + 64:i * 128 + 128]
                        first = (i % 4 == 0)
                        for (oo, lh, rh) in ((o_r, Gr, Ar), (o_r, Gi, nAi), (o_i, Gi, Ar), (o_i, Gr, Ai)):
                            tk.op('pe', lambda e: e.matmul(oo, lh, rh, start=first, stop=True, skip_group_check=True),
                                  reads=[('Gb', gs), 'Au', 'Ak'], writes=[('psd', d)])
                            first = False
                    q = kc % 2
                    pall = pd[:].rearrange("p (i r x) -> p i r x", i=8, r=2)
                    pu = pall[:, :, :, 0:CG]
                    kv = Kc[q][:].rearrange("p (i r c) -> p i r c", i=8, r=2)
                    tk.copy('act', kv, pall[:, :, :, CG:2 * CG], reads=[('psd', d)], writes=[('Kc', q)])
                    tk.op('dve', lambda e: e.tensor_tensor(out=P1[q][:].rearrange("p (i r c) -> p i r c", i=8, r=2),
                                                           in0=pu, in1=kv, op=ALU.mult),
                          reads=[('psd', d), ('Kc', q)], writes=[('P1', q)])
                    t3v = T3[q][:].rearrange("p (i c) -> p i c", i=8)
                    t4v = T4[q][:].rearrange("p (i c) -> p i c", i=8)
                    tk.op('dve', lambda e: e.tensor_tensor(out=t3v, in0=pu[:, :, 0, :], in1=kv[:, :, 1, :], op=ALU.mult),
                          reads=[('psd', d), ('Kc', q)], writes=[('T3', q)])
                    tk.op('dve', lambda e: e.tensor_tensor(out=t4v, in0=pu[:, :, 1, :], in1=kv[:, :, 0, :], op=ALU.mult),
                          reads=[('psd', d), ('Kc', q)], writes=[('T4', q)])
                    p1v = P1[q][:].rearrange("p (i r c) -> p c r i", i=8, r=2)
                    tk.op('pool', lambda e: e.tensor_tensor(out=Y[:, :, 0, kc * 8:(kc + 1) * 8], in0=p1v[:, :, 0, :],
                                                            in1=p1v[:, :, 1, :], op=ALU.subtract),
                          reads=[('P1', q)], writes=['Y'])
                    tk.op('pool', lambda e: e.tensor_tensor(out=Y[:, :, 1, kc * 8:(kc + 1) * 8],
                                                            in0=T3[q][:].rearrange("p (i c) -> p c i", i=8),
                                                            in1=T4[q][:].rearrange("p (i c) -> p c i", i=8), op=ALU.add),
                          reads=[('T3', q), ('T4', q)], writes=['Y'])
              tk.barrier()
              with ExitStack() as es2:
                sb2 = lambda name, shape, dt: es2.enter_context(nc.sbuf_tensor("h3b%d%d_" % (g, n) + name, shape, dt))
                yc = sb2("yc", [CG, L], F32)
                ufc = [sb2("ufc%d" % i, [CG, 2048], F32) for i in range(2)]
                gtc = [sb2("gtc%d" % i, [CG, 2048], F32) for i in range(2)]
                ybc = [sb2("ybc%d" % i, [CG, 2048], BF16) for i in range(2)]
                for cp in range(CG // 2):
                    b = pi % 8
                    pi += 1
                    for i in range(2):
                        c = cp * 2 + i
                        tk.op('pe', lambda e: e.matmul(ps[b][:, i * 256:(i + 1) * 256], Y[:, c, 0, :], F2[:, 0, :],
                                                       start=(i == 0), stop=False, skip_group_check=True),
                              reads=['Y', 'F2'], writes=[('ps', b)])
                        tk.op('pe', lambda e: e.matmul(ps[b][:, i * 256:(i + 1) * 256], Y[:, c, 1, :], F2[:, 1, :],
                                                       start=False, stop=True, skip_group_check=True),
                              reads=['Y', 'F2'], writes=[('ps', b)])
                    tk.copy(tk.alt(), Qs[:, :, :, cp * 2:cp * 2 + 2].rearrange("p b r c -> p c r b"),
                            ps[b][:].rearrange("p (c r b) -> p c r b", c=2, r=2), reads=[('ps', b)], writes=['Qs'])
                ycv = yc[:].rearrange("c (a b) -> c b a", b=128)
                for bc in range(16):
                    b = pi % 8
                    pi += 1
                    for i in range(8):
                        bb = bc * 8 + i
                        tk.op('pe', lambda e: e.matmul(ps[b][0:CG, i * 64:(i + 1) * 64], Qs[:, bb, 0, :], H[:, bb, 0, :],
                                                       start=(i == 0), stop=False, skip_group_check=True),
                              reads=['Qs'] + HR, writes=[('ps', b)])
                        tk.op('pe', lambda e: e.matmul(ps[b][0:CG, i * 64:(i + 1) * 64], Qs[:, bb, 1, :], H[:, bb, 1, :],
                                                       start=False, stop=True, skip_group_check=True),
                              reads=['Qs'] + HR, writes=[('ps', b)])
                    tk.op('act', lambda e: e.activation(out=ycv[:, bc * 8:(bc + 1) * 8, :],
                                                        in_=ps[b][0:CG, :].rearrange("c (i a) -> c i a", i=8),
                                                        func=AF.Copy, scale=1.0 / NFFT),
                          reads=[('ps', b)], writes=['yc'])
                for q4 in range(4):
                    s2 = q4 % 2
                    cs = slice(q4 * 2048, (q4 + 1) * 2048)
                    usrc32 = A['gd'][0] if n == 0 else A['y1f']
                    tk.dma('sp', ufc[s2][:], usrc32[c0:c0 + CG, cs], reads=[('gd', 0), 'y1f'], writes=[('ufc', s2)])
                    tk.dma('sp', gtc[s2][:], A['gd'][1 + n][c0:c0 + CG, cs], reads=[('gd', 1 + n)], writes=[('gtc', s2)])
                    tk.op('dve', lambda e: e.scalar_tensor_tensor(out=ufc[s2][:], in0=ufc[s2][:], scalar=hb[:, n:n + 1],
                                                                  in1=yc[:, cs], op0=ALU.mult, op1=ALU.add),
                          reads=[('ufc', s2), 'hb', 'yc'], writes=[('ufc', s2)])
                    if n == 0:
                        tk.op('dve', lambda e: e.tensor_tensor(out=ufc[s2][:], in0=ufc[s2][:], in1=gtc[s2][:], op=ALU.mult),
                              reads=[('ufc', s2), ('gtc', s2)], writes=[('ufc', s2)])
                        tk.copy('act', ybc[s2][:], ufc[s2][:], reads=[('ufc', s2)], writes=[('ybc', s2)])
                        tk.dma('sp', A['y1f'][c0:c0 + CG, cs], ufc[s2][:], reads=[('ufc', s2)], writes=['y1f'])
                        tk.dma('sp', A['y1d'][c0:c0 + CG, cs], ybc[s2][:], reads=[('ybc', s2)], writes=['y1d'])
                    else:
                        tk.op('dve', lambda e: e.tensor_tensor(out=ybc[s2][:], in0=ufc[s2][:], in1=gtc[s2][:], op=ALU.mult),
                              reads=[('ufc', s2), ('gtc', s2)], writes=[('ybc', s2)])
                        tk.dma('sp', A['y2d'][c0:c0 + CG, cs], ybc[s2][:], reads=[('ybc', s2)], writes=['y2d'])
              tk.barrier()
    tk.barrier()
    with ExitStack() as es:
        sb = lambda name, shape, dt: es.enter_context(nc.sbuf_tensor("h4_" + name, shape, dt))
        pst = [es.enter_context(nc.psum_tensor("h4ps%d" % i, [128, 1024], BF16)) for i in range(2)]
        y2 = sb("y2", [128, L], BF16)
        tk.dma('sp', y2[:], A['y2d'], reads=['y2d'], writes=['y2'])
        ot = [sb("ot%d" % i, [128, 8, 128], BF16) for i in range(2)]
        for tg in range(8):
            s = tg % 2
            for i in range(8):
                t = tg * 8 + i
                tk.op('pe', lambda e: e.transpose(out=pst[s][:, i * 128:(i + 1) * 128], in_=y2[:, t * 128:(t + 1) * 128],
                                                  identity=identb[:]), reads=['y2', 'ident'], writes=[('pst', s)])
            tk.copy(tk.alt(), ot[s][:], pst[s][:].rearrange("p (i c) -> p i c", i=8), reads=[('pst', s)],
                    writes=[('ot', s)])
            tk.dma('sp', A['a2a_in'][tg * 1024:(tg + 1) * 1024, 0:128].rearrange("(i p) c -> p i c", p=128), ot[s][:],
                   reads=[('ot', s)], writes=['a2a_in'])
    tk.barrier()


def phase_c(nc, tk, A, identb):
    with ExitStack() as es:
        sb = lambda name, shape, dt: es.enter_context(nc.sbuf_tensor("c_" + name, shape, dt))
        ps = [es.enter_context(nc.psum_tensor("cps%d" % i, [128, 512], F32)) for i in range(8)]
        Wo = sb("Wo", [128, 16, 2048], BF16)
        for i in range(4):
            tk.dma('pool', Wo[:, :, i * 512:(i + 1) * 512],
                   A['w_out'][:, i * 512:(i + 1) * 512].rearrange("(k p) n -> p k n", p=128), writes=[('Wo', i)])
        WR = [('Wo', i) for i in range(4)]
        hg = sb("hg", [128, 1024], F32)
        tk.dma('sp', hg[:], A['hyena_norm_g'].partition_broadcast(128), writes=['hg'])
        mx = [sb("mx%d" % i, [128, 8, 256], BF16) for i in range(2)]
        hy = sb("hy", [128, 8, 128], F32)
        hjunk = sb("hjunk", [128, 1024], F32)
        ss = sb("ss", [128, 1], F32)
        rstd = sb("rstd", [128, 1], F32)
        cat = [sb("cat%d" % i, [128, 2048], BF16) for i in range(2)]
        catT = [sb("catT%d" % i, [128, 16, 128], BF16) for i in range(2)]
        hres = [sb("hres%d" % i, [128, 2048], F32) for i in range(2)]
        zt = [sb("zt%d" % i, [128, 2048], F32) for i in range(2)]
        a2v = A['a2a_out'].rearrange("(i t) c -> t i c", i=8)
        cand = [sb("cand%d" % i, [128, 8, 256], BF16) for i in range(3)]
        sel = sb("sel", [128, 8], F32)
        tk.dma('sp', sel[:], A['sel'], writes=['sel'])
        for t in range(8):
            s = t % 2
            for jj in range(8):
                cs_ = jj % 3
                tk.dma('sp', cand[cs_][:], a2v[jj * TL + t * 128:jj * TL + (t + 1) * 128], reads=['a2a_out'],
                       writes=[('cand', cs_)])
                if jj == 0:
                    tk.op('dve', lambda e: e.tensor_scalar(out=mx[s][:], in0=cand[cs_][:], scalar1=sel[:, 0:1],
                                                           scalar2=None, op0=ALU.mult),
                          reads=[('cand', cs_), 'sel'], writes=[('mx', s)])
                else:
                    tk.op('dve', lambda e: e.scalar_tensor_tensor(out=mx[s][:], in0=cand[cs_][:], scalar=sel[:, jj:jj + 1],
                                                                  in1=mx[s][:], op0=ALU.mult, op1=ALU.add),
                          reads=[('cand', cs_), 'sel', ('mx', s)], writes=[('mx', s)])
            tk.dma('sp', hres[s][:], A['h_tok'][t * 128:(t + 1) * 128, :], reads=['h_tok'], writes=[('hres', s)])
            tk.copy('dve', hy[:], mx[s][:, :, 0:128], reads=[('mx', s)], writes=['hy'])
            hyf = hy[:].rearrange("p a b -> p (a b)")
            tk.op('dve', lambda e: e.scalar_tensor_tensor(out=hjunk[:], in0=hyf, scalar=1.0, in1=hyf,
                                                          op0=ALU.mult, op1=ALU.mult, accum_out=ss[:]),
                  reads=['hy'], writes=['hjunk', 'ss'])
            tk.op('act', lambda e: e.activation(out=rstd[:], in_=ss[:], func=AF.Sqrt, bias=LN_EPS, scale=1.0 / 1024),
                  reads=['ss'], writes=['rstd'])
            tk.op('dve', lambda e: e.reciprocal(out=rstd[:], in_=rstd[:]), reads=['rstd'], writes=['rstd'])
            tk.op('dve', lambda e: e.scalar_tensor_tensor(out=cat[s][:, 0:1024], in0=hyf, scalar=rstd[:, 0:1], in1=hg[:],
                                                          op0=ALU.mult, op1=ALU.mult),
                  reads=['hy', 'rstd', 'hg'], writes=[('cat', s, 0)])
            tk.copy('act', cat[s][:, 1024:2048].rearrange("p (a b) -> p a b", a=8), mx[s][:, :, 128:256],
                    reads=[('mx', s)], writes=[('cat', s, 1)])
            for half in range(2):
                bi = s * 2 + half
                pv = ps[bi][:].bitcast(BF16)
                for kk in range(8):
                    k = half * 8 + kk
                    tk.op('pe', lambda e: e.transpose(out=pv[:, kk * 128:(kk + 1) * 128],
                                                      in_=cat[s][:, k * 128:(k + 1) * 128], identity=identb[:]),
                          reads=[('cat', s, 0), ('cat', s, 1), 'ident'], writes=[('ps', bi)])
                tk.copy(tk.alt(), catT[s][:, half * 8:(half + 1) * 8, :], pv.rearrange("p (k n) -> p k n", k=8),
                        reads=[('ps', bi)], writes=[('catT', s, half)])
            tk.op('act', lambda e: e.activation(out=hres[s][:], in_=hres[s][:], func=AF.Copy, scale=ALPHA),
                  reads=[('hres', s)], writes=[('hres', s)])
            for c in range(4):
                for k in range(16):
                    tk.op('pe', lambda e: e.matmul(ps[4 + c][:], catT[s][:, k, :], Wo[:, k, c * 512:(c + 1) * 512],
                                                   start=(k == 0), stop=(k == 15)),
                          reads=WR + [('catT', s, 0), ('catT', s, 1)], writes=[('ps', 4 + c)])
                tk.op('dve', lambda e: e.tensor_tensor(out=zt[s][:, c * 512:(c + 1) * 512], in0=ps[4 + c][:],
                                                       in1=hres[s][:, c * 512:(c + 1) * 512], op=ALU.add),
                      reads=[('ps', 4 + c), ('hres', s)], writes=[('zt', s, c)])
            tk.dma('sp', A['z_dram'][t * 128:(t + 1) * 128, :], zt[s][:], reads=[('zt', s, c) for c in range(4)],
                   writes=[('zd', t)])
        layer_norm_rows(nc, tk, es, "c", A['z_dram'], A['ln2_g'], A['ln2_b'], A['x2_tok'], None, ps, identb)
    tk.barrier()


def build():
    nc = bass.Bass("TRN2", target_bir_lowering=False)
    A = {}

    def inp(name, shape, dt=F32):
        if TEST_MIXER and (name.startswith('ffn') or name in ('x', 'ln1_g', 'ln1_b', 'ln3_g', 'ln3_b')):
            return
        A[name] = nc.dram_tensor(name, shape, dt, kind="ExternalInput").ap()
        DECL.append(name)

    def tmp(name, shape, dt=F32):
        kind = {"kind": "ExternalOutput"} if name in DBG else {}
        A[name] = nc.dram_tensor(name, shape, dt, **kind).ap()

    inp('x', [TL, D])
    for f in ('ffn1', 'ffn2'):
        inp(f + '_w_gate', [D, DFF]); inp(f + '_w_up', [D, DFF]); inp(f + '_w_down', [DFF, D])
    for n in ('ln1', 'ln2', 'ln3'):
        inp(n + '_g', [D]); inp(n + '_b', [D])
    inp('w_in_c', [D, 1024]); inp('w_out', [D, D])
    inp('conv_w_c', [128, 9]); inp('conv_b_c', [128, 3])
    inp('fw1', [33, 64]); inp('fw2', [64, 64]); inp('fw3', [64, 64]); inp('fbf', [64, 4])
    inp('fw_out_c', [64, 512]); inp('hbias_c', [128, 2]); inp('hyena_norm_g', [1024])
    for n in ('lq1', 'lk1', 'lq2', 'lk2'):
        inp(n, [64])
    inp('subln_g', [128])
    inp('zk', [33, NFFT]); inp('decay_k', [128, NFFT])
    inp('cos_tab', [128, L]); inp('sin_tab', [128, L])
    inp('F1', [128, 256], BF16); inp('F2', [128, 2, 256], BF16)
    inp('G', [128, 128, 256], BF16); inp('H', [128, 128, 2, 64], BF16)
    inp('identf', [128, 128]); inp('identb', [128, 128], BF16); inp('sel', [128, 8])
    A['y'] = nc.dram_tensor('y', [TL, D], F32, kind="ExternalOutput").ap()
    tmp('z_dram', [TL, D]); tmp('h_tok', [TL, D]); tmp('x2_tok', [TL, D])
    ag1_in = nc.dram_tensor('ag1_in', [D, TL], BF16)
    h_allT = nc.dram_tensor('h_allT', [NCORES * D, TL], BF16)
    A['ag1_in'] = ag1_in.ap(); A['h_allT'] = h_allT.ap()
    tmp('phy', [3, 128, L]); tmp('gd', [3, 128, L]); tmp('ud', [128, L], BF16)
    tmp('kd0', [128, NFFT], BF16); tmp('kd1', [128, NFFT], BF16)
    A['kd'] = [A['kd0'], A['kd1']]
    tmp('y1d', [128, L], BF16); tmp('y2d', [128, L], BF16); tmp('y1f', [128, L])
    a2a_in = nc.dram_tensor('a2a_in', [L, 256], BF16)
    a2a_out = nc.dram_tensor('a2a_out', [NCORES * L, 256], BF16)
    A['a2a_in'] = a2a_in.ap(); A['a2a_out'] = a2a_out.ap()
    tmp('a2a_dbg', [L, 256], BF16)
    with ExitStack() as es:
        tk = TK(nc, es)
        cc = es.enter_context(nc.semaphore("ccsem"))
        identf = es.enter_context(nc.sbuf_tensor("identf_sb", [128, 128], F32))
        identb = es.enter_context(nc.sbuf_tensor("identb_sb", [128, 128], BF16))
        es.enter_context(nc.Block())
        tk.dma('sp', identf[:], A['identf'], writes=['ident'])
        tk.dma('sp', identb[:], A['identb'], writes=['ident'])
        if TEST_MIXER:
            inp('hT_in', [D, TL], BF16)
            inp('h_tok_in', [TL, D])
            tk.dma('sp', A['ag1_in'], A['hT_in'], writes=['ag1'])
            tk.dma('sp', A['h_tok'], A['h_tok_in'], writes=['h_tok'])
            tk.barrier()
        else:
          ffn_ln(nc, tk, "1", A['x'], A['ffn1_w_gate'], A['ffn1_w_up'], A['ffn1_w_down'], A['ln1_g'], A['ln1_b'],
               A['z_dram'], A['h_tok'], A['ag1_in'], identf, identb)
        ccn = 0
        if STAGE >= 2:
            nc.gpsimd.collective_compute("AllGather", ALU.bypass, replica_groups=[list(range(NCORES))],
                                         ins=[ag1_in.ap().opt()], outs=[h_allT.ap().opt()]).then_inc(cc)
            ccn += 1
            for e in tk.engs.values():
                e.wait_ge(cc, ccn)
            mixer(nc, tk, A, identb)
        if STAGE >= 3:
            hyena(nc, tk, A, identb)
        if 'a2a_dbg' in DBG:
            tk.dma('sp', A['a2a_dbg'], A['a2a_in'], reads=['a2a_in'], writes=['a2a_dbg'])
            tk.barrier()
        if STAGE >= 4:
            nc.gpsimd.collective_compute("AllGather", ALU.bypass, replica_groups=[list(range(NCORES))],
                                         ins=[a2a_in.ap().opt()], outs=[a2a_out.ap().opt()]).then_inc(cc)
            ccn += 1
            for e in tk.engs.values():
                e.wait_ge(cc, ccn)
            phase_c(nc, tk, A, identb)
            if not TEST_MIXER:
              ffn_ln(nc, tk, "2", A['x2_tok'], A['ffn2_w_gate'], A['ffn2_w_up'], A['ffn2_w_down'], A['ln3_g'],
                   A['ln3_b'], A['z_dram'], A['y'], None, identf, identb)
        tk.barrier()
    return nc


STAGE = 4
DBG = ()
TEST_MIXER = False
DECL = []
TEST_H = None
RUN_KW = {}


def host_consts():
    c = {}
    bf = ml_dtypes.bfloat16
    a = np.arange(128, dtype=np.float64)
    ang = 2 * np.pi * np.outer(a, a) / 128.0
    c['F1'] = np.concatenate([np.cos(ang), -np.sin(ang)], axis=1).astype(bf)
    Er, Ei = np.cos(ang), np.sin(ang)
    c['F2'] = np.stack([np.concatenate([Er, Ei], 1), np.concatenate([-Ei, Er], 1)], axis=1).astype(bf)
    b = a[:, None, None]; k1 = a[None, :, None]; k2 = a[None, None, :]
    ang = 2 * np.pi * (b * (k1 + 128 * k2) % NFFT) / NFFT
    c['G'] = np.concatenate([np.cos(ang), -np.sin(ang)], axis=2).astype(bf)
    k1 = a[:, None, None]; b = a[None, :, None]; aa = np.arange(64, dtype=np.float64)[None, None, :]
    ang = 2 * np.pi * (((128 * aa + b) * k1) % NFFT) / NFFT
    c['H'] = np.stack([np.cos(ang), -np.sin(ang)], axis=2).astype(bf)
    c['identf'] = np.eye(128, dtype=np.float32)
    c['identb'] = np.eye(128, dtype=np.float32).astype(bf)
    inv = (500000.0 ** (-np.arange(0, 16, 2, dtype=np.float32) / 16)).astype(np.float32)
    pos = np.arange(L, dtype=np.float32)
    angr = (pos[:, None] * inv[None, :]).astype(np.float32)
    cs, sn = np.cos(angr).astype(np.float32), np.sin(angr).astype(np.float32)
    cos_tab = np.ones((128, L), np.float32); sin_tab = np.zeros((128, L), np.float32)
    for m in range(2):
        for d in range(16):
            cos_tab[m * 64 + d] = cs[:, d % 8]
            sin_tab[m * 64 + d] = (-sn[:, d] if d < 8 else sn[:, d - 8])
    c['cos_tab'] = cos_tab; c['sin_tab'] = sin_tab
    t = np.linspace(0.0, 1.0, L, dtype=np.float32)
    w = (2.0 * np.float32(math.pi) * np.arange(L, dtype=np.float32) / np.float32(L)).astype(np.float32)
    f = np.linspace(1e-4, 15.0, 16, dtype=np.float32)
    fw = (f[None, :] * w[:, None]).astype(np.float32)
    z = np.concatenate([t[:, None], np.cos(fw), -np.sin(fw)], axis=1).astype(np.float32)
    idx = np.concatenate([np.arange(L), [0], np.arange(L - 1, 0, -1)])
    c['zk'] = np.ascontiguousarray(z[idx].T)
    min_decay = math.log(1e-2) / 1.5
    max_decay = math.log(1e-2) / 0.3
    deltas = np.abs(np.linspace(min_decay, max_decay, 1024, dtype=np.float32))
    decay = np.exp(-t[:, None] * deltas[None, :]).astype(np.float32)
    dk = decay[idx]
    dk[L] = 0.0
    c['decay_all'] = np.ascontiguousarray(dk.T)
    return c


def kernel(**inputs):
    I = {k: np.asarray(v) for k, v in inputs.items()}
    hc = host_consts()
    del DECL[:]
    nc = build()
    w_in = I['w_in'][0]
    cw = I['hyena_conv_w'][0]; cbv = I['hyena_conv_b'][0]
    fwo = I['filt_w_out'][0]
    permd = np.arange(128)
    for m in range(2):
        for d in range(16):
            permd[m * 64 + d] = m * 64 + (d + 8 if d < 8 else d - 8)
    common = {
        'w_out': I['w_out'][0], 'hyena_norm_g': I['hyena_norm_g'][0],
        'fw1': I['filt_w1'][0], 'fw2': I['filt_w2'][0], 'fw3': I['filt_w3'][0],
        'fbf': np.ascontiguousarray(np.stack([I['filt_b1'][0], I['filt_b2'][0], I['filt_b3'][0], I['filt_freq'][0]], 1)),
        'lq1': I['lambda_q1'][0], 'lk1': I['lambda_k1'][0], 'lq2': I['lambda_q2'][0], 'lk2': I['lambda_k2'][0],
        'subln_g': I['subln_g'][0],
    }
    for f in ('ffn1', 'ffn2'):
        for s in ('_w_gate', '_w_up', '_w_down'):
            common[f + s] = I[f + s][0]
    for n in ('ln1', 'ln2', 'ln3'):
        common[n + '_g'] = I[n + '_g'][0]; common[n + '_b'] = I[n + '_b'][0]
    for k in ('zk', 'cos_tab', 'sin_tab', 'F1', 'F2', 'G', 'H', 'identf', 'identb'):
        common[k] = hc[k]
    in_maps = []
    for i in range(NCORES):
        sl = slice(i * 128, (i + 1) * 128)
        qc = w_in[:, 3072 + i * 128:3072 + (i + 1) * 128]
        kc = w_in[:, 4096 + i * 128:4096 + (i + 1) * 128]
        w_in_c = np.concatenate([w_in[:, sl], w_in[:, 1024 + i * 128:1024 + (i + 1) * 128],
                                 w_in[:, 2048 + i * 128:2048 + (i + 1) * 128], qc, kc,
                                 w_in[:, 5120 + i * 128:5120 + (i + 1) * 128], qc[:, permd], kc[:, permd]], axis=1)
        conv_w_c = np.stack([cw[tap, m * 1024 + i * 128:m * 1024 + (i + 1) * 128] for tap in range(3) for m in range(3)], 1)
        conv_b_c = np.stack([cbv[m * 1024 + i * 128:m * 1024 + (i + 1) * 128] for m in range(3)], 1)
        fw_out_c = np.concatenate([fwo[:, q * 1024 + i * 128:q * 1024 + (i + 1) * 128] for q in range(4)], 1)
        d = dict(common)
        d.update({
            'x': np.ascontiguousarray(I['x'][0, i * TL:(i + 1) * TL]),
            'w_in_c': np.ascontiguousarray(w_in_c), 'conv_w_c': np.ascontiguousarray(conv_w_c),
            'conv_b_c': np.ascontiguousarray(conv_b_c), 'fw_out_c': np.ascontiguousarray(fw_out_c),
            'hbias_c': np.ascontiguousarray(I['hyena_bias'][0][:, sl].T),
            'decay_k': np.ascontiguousarray(hc['decay_all'][sl]),
            'sel': np.ascontiguousarray(np.tile(np.eye(8, dtype=np.float32)[i][None, :], (128, 1))),
        })
        if TEST_MIXER:
            d['hT_in'] = TEST_H[i][0]; d['h_tok_in'] = TEST_H[i][1]
        in_maps.append({k: d[k] for k in DECL})
    res = run_bass_kernel_spmd(nc, in_maps, core_ids=list(range(NCORES)), **RUN_KW)
    kernel.last = res
    if TEST_MIXER:
        return None
    out = np.concatenate([np.asarray(r['y']) for r in res.results], axis=0)
    return out.reshape(1, L, D).astype(np.float32)
```
